# Optimizing a Trainium2 kernel written in Bass

```python
import math
import jax, jax.numpy as jnp
from jax import lax
import numpy as np

D_MODEL = 1024
BATCH = 8
SEQ = 8192
DEPTH = 2

ATT_HEADS = 8
ATT_HEAD_DIM = 64
ATT_WIDTH = ATT_HEADS * ATT_HEAD_DIM
POOL_GROUPS = 4
POOL_GROUP_DIM = 64
POOL_WIDTH = POOL_GROUPS * POOL_GROUP_DIM
POOL_WINDOWS = (2, 4, 8, 16)
HGRN_HEADS = 4
HGRN_KEY_DIM = 64
HGRN_VAL_DIM = 64
HGRN_WIDTH = HGRN_HEADS * HGRN_KEY_DIM
MIX_WIDTH = ATT_WIDTH + POOL_WIDTH + HGRN_HEADS * HGRN_VAL_DIM
DILATED_GROUPS = ((128, 1), (512, 4), (2048, 16))
NUM_BUCKETS = 32
MAX_DISTANCE = 1024
HGRN_CHUNK = 64
D_FF = 2816
IN_SPLITS = (ATT_WIDTH, ATT_WIDTH, ATT_WIDTH, POOL_WIDTH,
             HGRN_WIDTH, HGRN_HEADS * HGRN_VAL_DIM, HGRN_WIDTH, HGRN_WIDTH,
             HGRN_HEADS * HGRN_VAL_DIM)
IN_COLS = sum(IN_SPLITS)
EPS = 1e-6
NEG = -1e30

kernel_name = "hybrid_dilated_pool_hgrn2_encoder"


def rmsnorm(x, g):
    xf = x.astype(jnp.float32)
    return xf * lax.rsqrt(jnp.mean(xf * xf, axis=-1, keepdims=True) + EPS) * g.astype(jnp.float32)


def swiglu(h, w_gate, w_up, w_down):
    return (jax.nn.silu(h @ w_gate) * (h @ w_up)) @ w_down


def t5_bucket(rel):
    half = NUM_BUCKETS // 2
    max_exact = half // 2
    base = jnp.where(rel > 0, half, 0)
    n = jnp.abs(rel)
    nf = jnp.maximum(n, 1).astype(jnp.float32)
    large = max_exact + (jnp.log(nf / max_exact) / math.log(MAX_DISTANCE / max_exact)
                         * (half - max_exact)).astype(jnp.int32)
    large = jnp.minimum(large, half - 1)
    return base + jnp.where(n < max_exact, n, large)


def group_bias(table, side, dil):
    qi = jnp.arange(side)[:, None]
    ki = jnp.arange(3 * side)[None, :]
    rel = (ki - side - qi) * dil
    return table[t5_bucket(rel)].astype(jnp.float32).transpose(2, 0, 1)


def dilated_group(q, k, v, bias, side, dil):
    B, H, S, hd = q.shape
    W = side
    L = S // dil
    nb = -(-L // W)
    Lp = nb * W

    def sub(t):
        return t.reshape(B, H, L, dil, hd).transpose(0, 1, 3, 2, 4)

    qb = jnp.pad(sub(q), ((0, 0), (0, 0), (0, 0), (0, Lp - L), (0, 0))).reshape(B, H, dil, nb, W, hd)

    def band(t):
        tp = jnp.pad(sub(t), ((0, 0), (0, 0), (0, 0), (W, Lp - L + W), (0, 0)))
        return jnp.concatenate(
            [tp[:, :, :, j * W:j * W + Lp, :].reshape(B, H, dil, nb, W, hd) for j in range(3)], axis=-2)

    kb, vb = band(k), band(v)
    s = jnp.einsum('bhrnqd,bhrnkd->bhrnqk', qb, kb) * (hd ** -0.5) + bias[:, None, None]
    qi = jnp.arange(W)[:, None]
    ki = jnp.arange(3 * W)[None, :]
    rel = ki - W - qi
    lk = jnp.arange(nb)[:, None, None] * W + ki[None] - W
    valid = (jnp.abs(rel) <= W)[None] & (lk >= 0) & (lk < L)
    s = jnp.where(valid, s, NEG)
    m = jnp.max(s, axis=-1, keepdims=True)
    p = jnp.exp(s - m)
    l = jnp.sum(p, axis=-1)
    o = jnp.einsum('bhrnqk,bhrnkd->bhrnqd', p, vb) / l[..., None]
    o = o.reshape(B, H, dil, Lp, hd)[:, :, :, :L].transpose(0, 1, 3, 2, 4).reshape(B, H, S, hd)
    m = m[..., 0].reshape(B, H, dil, Lp)[..., :L].transpose(0, 1, 3, 2).reshape(B, H, S)
    l = l.reshape(B, H, dil, Lp)[..., :L].transpose(0, 1, 3, 2).reshape(B, H, S)
    return o, m, l


def dilated_attention(zq, zk, zv, q_gain, k_gain, biases):
    B, S, _ = zq.shape
    shp = (B, S, ATT_HEADS, ATT_HEAD_DIM)
    q = rmsnorm(zq.reshape(shp), q_gain).transpose(0, 2, 1, 3)
    k = rmsnorm(zk.reshape(shp), k_gain).transpose(0, 2, 1, 3)
    v = zv.reshape(shp).astype(jnp.float32).transpose(0, 2, 1, 3)
    outs, ms, ls = [], [], []
    for (win, dil), bias in zip(DILATED_GROUPS, biases):
        o, m, l = dilated_group(q, k, v, bias, (win // 2) // dil, dil)
        outs.append(o); ms.append(m); ls.append(l)
    ms = jnp.stack(ms); ls = jnp.stack(ls); outs = jnp.stack(outs)
    wts = ls * jnp.exp(ms - jnp.max(ms, axis=0, keepdims=True))
    o = jnp.sum(wts[..., None] * outs, axis=0) / jnp.sum(wts, axis=0)[..., None]
    return o.transpose(0, 2, 1, 3).reshape(B, S, ATT_WIDTH)


def pool_mixer(u, w_pool, scale):
    B, S, _ = u.shape
    pos = jnp.arange(S)
    outs = []
    for gi, win in enumerate(POOL_WINDOWS):
        ug = u[..., gi * POOL_GROUP_DIM:(gi + 1) * POOL_GROUP_DIM].astype(jnp.float32)
        cs = jnp.concatenate([jnp.zeros((B, 1, POOL_GROUP_DIM), jnp.float32), jnp.cumsum(ug, axis=1)], axis=1)
        lo = jnp.clip(pos - win // 2, 0, S)
        hi = jnp.clip(pos + win - win // 2, 0, S)
        mean = (cs[:, hi] - cs[:, lo]) / (hi - lo).astype(jnp.float32)[None, :, None]
        outs.append(jnp.einsum('bsc,cd->bsd', mean - ug, w_pool[gi]))
    return jnp.concatenate(outs, axis=-1) * scale


def hgrn2_bidirectional(hq, hi, hf_fwd, hf_bwd, hgate, lb_fwd, lb_bwd, out_gain):
    B, S, _ = hq.shape
    C = HGRN_CHUNK
    N = S // C
    f32 = jnp.float32

    def gates(fl, lb):
        fl = fl.astype(f32)
        log_f = jnp.logaddexp(jnp.log(lb), jnp.log1p(-lb) + jax.nn.log_sigmoid(fl))
        k = (1.0 - lb) * jax.nn.sigmoid(-fl)
        return log_f, k

    g_f, k_f = gates(hf_fwd, lb_fwd)
    g_b, k_b = gates(hf_bwd, lb_bwd)
    flip = lambda t: jnp.flip(t, axis=1)
    qf, vf = hq.astype(f32), hi.astype(f32)

    def heads(t, d):
        return t.reshape(2, B, S, HGRN_HEADS, d).transpose(0, 1, 3, 2, 4)

    Q = heads(jnp.stack([qf, flip(qf)]), HGRN_KEY_DIM)
    V = heads(jnp.stack([vf, flip(vf)]), HGRN_VAL_DIM)
    G = heads(jnp.stack([g_f, flip(g_b)]), HGRN_KEY_DIM)
    K = heads(jnp.stack([k_f, flip(k_b)]), HGRN_KEY_DIM)

    def chunks(t):
        return jnp.moveaxis(t.reshape(t.shape[:3] + (N, C, t.shape[-1])), 3, 0)

    tri = jnp.tril(jnp.ones((C, C), dtype=bool))[:, :, None]

    def step(state, inp):
        qc, kc, vc, gc = inp
        b = jnp.cumsum(gc, axis=-2)
        o_inter = jnp.einsum('...tk,...kv->...tv', qc * jnp.exp(b), state)
        diff = b[..., :, None, :] - b[..., None, :, :]
        decay = jnp.exp(jnp.where(tri, diff, -jnp.inf))
        att = jnp.einsum('...tk,...sk,...tsk->...ts', qc, kc, decay)
        o_intra = jnp.einsum('...ts,...sv->...tv', att, vc)
        b_last = b[..., -1:, :]
        new_state = (jnp.swapaxes(jnp.exp(b_last), -1, -2) * state
                     + jnp.einsum('...sk,...sv->...kv', kc * jnp.exp(b_last - b), vc))
        return new_state, o_inter + o_intra

    s0 = jnp.zeros((2, B, HGRN_HEADS, HGRN_KEY_DIM, HGRN_VAL_DIM), f32)
    _, O = lax.scan(step, s0, (chunks(Q), chunks(K), chunks(V), chunks(G)))
    O = jnp.moveaxis(O, 0, 3).reshape(2, B, HGRN_HEADS, S, HGRN_VAL_DIM).transpose(0, 1, 3, 2, 4)
    o = O[0] + flip(O[1])
    o = rmsnorm(o, out_gain) * jax.nn.silu(hgate.astype(f32).reshape(B, S, HGRN_HEADS, HGRN_VAL_DIM))
    return o.reshape(B, S, HGRN_HEADS * HGRN_VAL_DIM)


def setup_inputs(seed: int = 0) -> dict:
    key = jax.random.key(seed)
    ks = jax.random.split(key, 20)
    nrm = lambda k, shape, sc: jax.random.normal(k, shape, jnp.float32) * sc
    return {
        "x": nrm(ks[0], (BATCH, SEQ, D_MODEL), 1.0),
        "ffn1_norm": 1.0 + nrm(ks[1], (DEPTH, D_MODEL), 0.1),
        "ffn1_w_gate": nrm(ks[2], (DEPTH, D_MODEL, D_FF), D_MODEL ** -0.5),
        "ffn1_w_up": nrm(ks[3], (DEPTH, D_MODEL, D_FF), D_MODEL ** -0.5),
        "ffn1_w_down": nrm(ks[4], (DEPTH, D_FF, D_MODEL), D_FF ** -0.5),
        "mix_norm": 1.0 + nrm(ks[5], (DEPTH, D_MODEL), 0.1),
        "w_in": nrm(ks[6], (DEPTH, D_MODEL, IN_COLS), D_MODEL ** -0.5),
        "q_norm": 1.0 + nrm(ks[7], (DEPTH, ATT_HEADS, ATT_HEAD_DIM), 0.1),
        "k_norm": 1.0 + nrm(ks[8], (DEPTH, ATT_HEADS, ATT_HEAD_DIM), 0.1),
        "rel_bias": nrm(ks[9], (NUM_BUCKETS, ATT_HEADS), 0.5),
        "pool_w": nrm(ks[10], (DEPTH, POOL_GROUPS, POOL_GROUP_DIM, POOL_GROUP_DIM), POOL_GROUP_DIM ** -0.5),
        "pool_scale": 1.0 + nrm(ks[11], (DEPTH, POOL_WIDTH), 0.1),
        "hgrn_lb_logits": nrm(ks[12], (2, DEPTH, HGRN_WIDTH), 1.0),
        "hgrn_norm": 1.0 + nrm(ks[13], (DEPTH, HGRN_VAL_DIM), 0.1),
        "w_out": nrm(ks[14], (DEPTH, MIX_WIDTH, D_MODEL), MIX_WIDTH ** -0.5),
        "ffn2_norm": 1.0 + nrm(ks[15], (DEPTH, D_MODEL), 0.1),
        "ffn2_w_gate": nrm(ks[16], (DEPTH, D_MODEL, D_FF), D_MODEL ** -0.5),
        "ffn2_w_up": nrm(ks[17], (DEPTH, D_MODEL, D_FF), D_MODEL ** -0.5),
        "ffn2_w_down": nrm(ks[18], (DEPTH, D_FF, D_MODEL), D_FF ** -0.5),
    }


def reference(x, ffn1_norm, ffn1_w_gate, ffn1_w_up, ffn1_w_down, mix_norm, w_in,
              q_norm, k_norm, rel_bias, pool_w, pool_scale, hgrn_lb_logits, hgrn_norm,
              w_out, ffn2_norm, ffn2_w_gate, ffn2_w_up, ffn2_w_down):
    in_dtype = x.dtype
    h_res = x.astype(jnp.float32)
    biases = [group_bias(rel_bias, (win // 2) // dil, dil) for win, dil in DILATED_GROUPS]
    lb_cum = jnp.cumsum(jax.nn.softmax(hgrn_lb_logits.astype(jnp.float32), axis=1), axis=1)
    lb_all = lb_cum - lb_cum[:, :1]
    split_idx = [int(v) for v in np.cumsum(IN_SPLITS)[:-1]]
    for l in range(DEPTH):
        h_res = h_res + 0.5 * swiglu(rmsnorm(h_res, ffn1_norm[l]), ffn1_w_gate[l], ffn1_w_up[l], ffn1_w_down[l])
        z = rmsnorm(h_res, mix_norm[l]) @ w_in[l]
        zq, zk, zv, zp, hq, hi, hff, hfb, hg = jnp.split(z, split_idx, axis=-1)
        y_att = dilated_attention(zq, zk, zv, q_norm[l], k_norm[l], biases)
        y_pool = pool_mixer(zp, pool_w[l], pool_scale[l])
        y_rec = hgrn2_bidirectional(hq, hi, hff, hfb, hg, lb_all[0, l], lb_all[1, l], hgrn_norm[l])
        h_res = h_res + jnp.concatenate([y_att, y_pool, y_rec], axis=-1) @ w_out[l]
        h_res = h_res + 0.5 * swiglu(rmsnorm(h_res, ffn2_norm[l]), ffn2_w_gate[l], ffn2_w_up[l], ffn2_w_down[l])
    return h_res.astype(in_dtype)
```

```python
import numpy as np
import concourse.bass as bass
import concourse.mybir as mybir
from concourse.bass_utils import run_bass_kernel_spmd

F32 = mybir.dt.float32
BF16 = mybir.dt.bfloat16
AF = mybir.ActivationFunctionType
ALU = mybir.AluOpType
AX = mybir.AxisListType

ENGS = ("pe", "act", "dve", "pool", "sp")
DMA_POOL = 8


class Res:
    __slots__ = ("name", "writer", "readers")

    def __init__(self, name=""):
        self.name = name
        self.writer = None
        self.readers = []


class Op:
    __slots__ = ("eng", "fn", "deps", "needed", "count", "is_dma", "dma_sem", "dma_target", "dma_prev", "gid")

    def __init__(self, eng, fn, is_dma):
        self.eng = eng
        self.fn = fn
        self.deps = []
        self.needed = False
        self.count = 0
        self.is_dma = is_dma
        self.dma_sem = None
        self.dma_target = 0
        self.dma_prev = 0
        self.gid = 0


class Sched:
    def __init__(self, nc):
        self.nc = nc
        self.ops = {e: [] for e in ENGS}
        self.n = 0
        self.dma_rr = {e: 0 for e in ENGS}
        self.dma_tgt = {}
        self.dma_since_barrier = []

    def add(self, eng, fn, reads=(), writes=(), dma=False):
        op = Op(eng, fn, dma)
        op.gid = self.n
        self.n += 1
        raw = set()
        other = set()
        for r in reads:
            if r.writer is not None:
                raw.add(r.writer)
        for w in writes:
            if w.writer is not None:
                other.add(w.writer)
            for rd in w.readers:
                other.add(rd)
        deps = set()
        for d in raw | other:
            if d is op:
                continue
            if d.is_dma or dma:
                deps.add(d)
            elif d.eng == eng:
                if eng == "pe":
                    continue
                deps.add(d)
            else:
                deps.add(d)
        op.deps = list(deps)
        for d in op.deps:
            d.needed = True
        for r in reads:
            r.readers.append(op)
        for w in writes:
            w.writer = op
            w.readers = []
        if dma:
            k = self.dma_rr[eng]
            self.dma_rr[eng] = (k + 1) % DMA_POOL
            key = (eng, k)
            prev = self.dma_tgt.get(key, 0)
            op.dma_sem = key
            op.dma_prev = prev
            op.dma_target = prev + 16
            self.dma_tgt[key] = prev + 16
            self.dma_since_barrier.append(op)
        self.ops[eng].append(op)
        return op

    def barrier(self):
        lasts = []
        for e in ENGS:
            for op in reversed(self.ops[e]):
                if not op.is_dma:
                    lasts.append(op)
                    break
        dmas = list(self.dma_since_barrier)
        self.dma_since_barrier = []
        for e in ENGS:
            op = Op(e, None, False)
            op.gid = self.n
            self.n += 1
            op.deps = [d for d in lasts if d.eng != e] + dmas
            for d in op.deps:
                d.needed = True
            self.ops[e].append(op)

    def emit(self):
        nc = self.nc
        for e in ENGS:
            c = 0
            for op in self.ops[e]:
                if op.is_dma or op.fn is None:
                    continue
                if op.needed:
                    c += 1
                    op.count = c
        import contextlib
        with contextlib.ExitStack() as st:
            esem = {e: st.enter_context(nc.semaphore("s_" + e)) for e in ENGS}
            dsem = {}
            for e in ("sp", "act", "pool"):
                for k in range(DMA_POOL):
                    dsem[(e, k)] = st.enter_context(nc.semaphore("d_%s%d" % (e, k)))
            block = st.enter_context(nc.Block())
            handles = {"pe": block.tensor, "act": block.scalar, "dve": block.vector,
                       "pool": block.gpsimd, "sp": block.sync}
            final_dma = dict(self.dma_tgt)

            def make(ename):
                def body(eng):
                    known = {}

                    def wait(sem_key, sem, val):
                        if val <= 0:
                            return
                        if known.get(sem_key, 0) >= val:
                            return
                        eng.wait_ge(sem, val)
                        known[sem_key] = val

                    for op in self.ops[ename]:
                        for d in op.deps:
                            if d.is_dma:
                                wait(d.dma_sem, dsem[d.dma_sem], d.dma_target)
                            else:
                                wait(d.eng, esem[d.eng], d.count)
                        if op.fn is None:
                            continue
                        if op.is_dma:
                            wait(op.dma_sem, dsem[op.dma_sem], op.dma_prev)
                            ins = op.fn(eng)
                            ins.then_inc(dsem[op.dma_sem], 16)
                        else:
                            ins = op.fn(eng)
                            if op.needed:
                                ins.then_inc(esem[ename], 1)
                    if ename == "sp":
                        for key, tgt in final_dma.items():
                            wait(key, dsem[key], tgt)
                return body

            for e in ENGS:
                handles[e](make(e))


class Arena:
    def __init__(self, ap_f32):
        self.t = ap_f32
        self.cap = ap_f32.shape[1] * 4
        self.off = 0
        self.peak = 0

    def alloc(self, shape, dtype, parts=128):
        n = int(np.prod(shape))
        sz = 2 if dtype == BF16 else 4
        nbytes = (n * sz + 63) // 64 * 64
        assert self.off + nbytes <= self.cap, ("arena overflow", self.off, nbytes, self.cap)
        v = self.t[0:parts, self.off // 4:(self.off + nbytes) // 4]
        self.off += nbytes
        self.peak = max(self.peak, self.off)
        if dtype == BF16:
            v = v.bitcast(BF16)
        v = v[:, 0:n]
        if len(shape) == 2:
            v = v.rearrange("p (a b) -> p a b", b=shape[1])
        elif len(shape) == 3:
            v = v.rearrange("p (a b c) -> p a b c", b=shape[1], c=shape[2])
        return v

    def mark(self):
        return self.off

    def release(self, m):
        self.off = m


D = 1024
DFF = 2816
NKC = D // 128
NFC = DFF // 128
EPS = 1e-6


class Ctx:
    pass


def cast_op(S, eng, out, in_, reads, writes):
    if eng == "act":
        return S.add("act", lambda e: e.copy(out, in_), reads, writes)
    return S.add(eng, lambda e: e.tensor_copy(out, in_), reads, writes)


def load_weight_rows(S, A, jobs):
    m = A.mark()
    NST = 3
    stg = [A.alloc((2816,), F32) for _ in range(NST)]
    r_stg = [Res("stg%d" % i) for i in range(NST)]
    engs = ("dve", "pool", "act")
    for j, (src, dst, rdst, shape) in enumerate(jobs):
        sb = j % NST
        n = int(np.prod(shape))
        sv = stg[sb][:, 0:n]
        if len(shape) == 2:
            sv = sv.rearrange("p (a b) -> p a b", b=shape[1])
        S.add("sp", (lambda e, sv=sv, src=src: e.dma_start(out=sv, in_=src)), [], [r_stg[sb]], dma=True)
        cast_op(S, engs[j % 3], dst, sv, [r_stg[sb]], [rdst])
    A.release(m)


def ffn_phase(S, A, P, K, x_src, x_dst, g_sb, wg_d, wu_d, wd_d, T, TT=256):
    m0 = A.mark()
    NS = TT // 128
    NT = T // TT
    wg = A.alloc((NKC, DFF), BF16)
    wu = A.alloc((NKC, DFF), BF16)
    wd = A.alloc((NFC, D), BF16)
    r_wg = [Res() for _ in range(NKC)]
    r_wu = [Res() for _ in range(NKC)]
    r_wd = [Res() for _ in range(NFC // 2)]
    jobs = []
    for kc in range(NKC):
        jobs.append((wg_d[kc * 128:(kc + 1) * 128, :], wg[:, kc, :], r_wg[kc], (DFF,)))
        jobs.append((wu_d[kc * 128:(kc + 1) * 128, :], wu[:, kc, :], r_wu[kc], (DFF,)))
    for c2 in range(NFC // 2):
        src = wd_d[c2 * 256:(c2 + 1) * 256, :].rearrange("(s p) d -> p s d", p=128)
        jobs.append((src, wd[:, 2 * c2:2 * c2 + 2, :], r_wd[c2], (2, D)))
    load_weight_rows(S, A, jobs)
    S.barrier()

    xt = [A.alloc((NS, D), F32) for _ in range(2)]
    r_xt = [Res() for _ in range(2)]
    hb = A.alloc((NS, D), BF16)
    r_hb = Res()
    hT = [A.alloc((NKC, TT), BF16) for _ in range(2)]
    r_hT = [[Res() for _ in range(NKC)] for _ in range(2)]
    aT = A.alloc((NFC, TT), BF16)
    r_aT = [Res() for _ in range(NFC)]
    sg = [A.alloc((TT,), F32) for _ in range(2)]
    r_sg = [Res() for _ in range(2)]
    junk = A.alloc((D,), BF16)
    r_junk = Res()
    ss = A.alloc((2 * NS,), F32)
    r_ss = Res()
    rs = A.alloc((2 * NS,), F32)
    r_rs = Res()
    sd = A.alloc((2 * NS,), F32)
    r_sd = Res()
    pT = [P.bf[0], P.bf[1]]
    r_pT = [P.r[0], P.r[1]]
    pG = [P.f[2], P.f[3]]
    r_pG = [P.r[2], P.r[3]]
    pU = [P.f[4], P.f[5]]
    r_pU = [P.r[4], P.r[5]]
    pO = [P.f[6], P.f[7]]
    r_pO = [P.r[6], P.r[7]]
    cnt = {"t": 0, "o": 0, "g": 0}

    def stageA(i):
        b = i % 2
        t0 = i * TT
        src = x_src[t0:t0 + TT, :].rearrange("(s p) d -> p s d", p=128)
        S.add("sp", lambda e: e.dma_start(out=xt[b], in_=src), [], [r_xt[b]], dma=True)
        for s in range(NS):
            S.add("act", lambda e, s=s: e.activation(out=junk, in_=xt[b][:, s, :], func=AF.Square,
                                                     accum_out=ss[:, b * NS + s:b * NS + s + 1]),
                  [r_xt[b]], [r_junk, r_ss])
        S.add("act", lambda e: e.activation(out=sd[:, b * NS:(b + 1) * NS], in_=ss[:, b * NS:(b + 1) * NS],
                                            func=AF.Sqrt, scale=1.0 / D, bias=K.eps_col),
              [r_ss, K.r_const], [r_sd])
        S.add("dve", lambda e: e.reciprocal(out=rs[:, b * NS:(b + 1) * NS], in_=sd[:, b * NS:(b + 1) * NS]),
              [r_sd], [r_rs])
        for s in range(NS):
            S.add("dve", lambda e, s=s: e.tensor_scalar(out=hb[:, s, :], in0=xt[b][:, s, :],
                                                        scalar1=rs[:, b * NS + s:b * NS + s + 1], scalar2=None,
                                                        op0=ALU.mult),
                  [r_xt[b], r_rs], [r_hb])
        for kc in range(NKC):
            q = cnt["t"] % 2
            cnt["t"] += 1
            for s in range(NS):
                S.add("pe", lambda e, s=s, kc=kc, q=q: e.transpose(out=pT[q][:, s * 128:(s + 1) * 128],
                                                                  in_=hb[:, s, kc * 128:(kc + 1) * 128],
                                                                  identity=K.ident),
                      [r_hb, K.r_const], [r_pT[q]])
            if kc % 2 == 0:
                S.add("dve", lambda e, kc=kc, q=q: e.tensor_scalar(out=hT[b][:, kc, :], in0=pT[q][:, 0:TT],
                                                                   scalar1=g_sb[:, kc:kc + 1], scalar2=None, op0=ALU.mult),
                      [r_pT[q], K.r_const], [r_hT[b][kc]])
            else:
                S.add("act", lambda e, kc=kc, q=q: e.activation(out=hT[b][:, kc, :], in_=pT[q][:, 0:TT], func=AF.Copy,
                                                                scale=g_sb[:, kc:kc + 1]),
                      [r_pT[q], K.r_const], [r_hT[b][kc]])

    def stageB1(i):
        b = i % 2
        for c in range(NFC):
            q = cnt["g"] % 2
            cnt["g"] += 1
            for kc in range(NKC):
                S.add("pe", lambda e, c=c, kc=kc, q=q: e.matmul(pG[q][:, 0:TT], lhsT=wg[:, kc, c * 128:(c + 1) * 128],
                                                               rhs=hT[b][:, kc, :], start=(kc == 0), stop=(kc == NKC - 1)),
                      [r_wg[kc], r_hT[b][kc]], [r_pG[q]])
            for kc in range(NKC):
                S.add("pe", lambda e, c=c, kc=kc, q=q: e.matmul(pU[q][:, 0:TT], lhsT=wu[:, kc, c * 128:(c + 1) * 128],
                                                               rhs=hT[b][:, kc, :], start=(kc == 0), stop=(kc == NKC - 1)),
                      [r_wu[kc], r_hT[b][kc]], [r_pU[q]])
            S.add("act", lambda e, q=q: e.activation(out=sg[q], in_=pG[q][:, 0:TT], func=AF.Silu),
                  [r_pG[q]], [r_sg[q]])
            S.add("dve", lambda e, c=c, q=q: e.tensor_tensor(out=aT[:, c, :], in0=pU[q][:, 0:TT], in1=sg[q], op=ALU.mult),
                  [r_pU[q], r_sg[q]], [r_aT[c]])

    def stageB2(i):
        b = i % 2
        t0 = i * TT
        for s in range(NS):
            for dh in range(2):
                q = cnt["o"] % 2
                cnt["o"] += 1
                for c in range(NFC):
                    S.add("pe", lambda e, c=c, s=s, dh=dh, q=q: e.matmul(pO[q], lhsT=aT[:, c, s * 128:(s + 1) * 128],
                                                                       rhs=wd[:, c, dh * 512:(dh + 1) * 512],
                                                                       start=(c == 0), stop=(c == NFC - 1)),
                          [r_aT[c], r_wd[c // 2]], [r_pO[q]])
                S.add("dve", lambda e, s=s, dh=dh, q=q: e.scalar_tensor_tensor(
                    out=xt[b][:, s, dh * 512:(dh + 1) * 512], in0=pO[q], scalar=0.5,
                    in1=xt[b][:, s, dh * 512:(dh + 1) * 512], op0=ALU.mult, op1=ALU.add),
                    [r_pO[q], r_xt[b]], [r_xt[b]])
        dst = x_dst[t0:t0 + TT, :].rearrange("(s p) d -> p s d", p=128)
        S.add("sp", lambda e: e.dma_start(out=dst, in_=xt[b]), [r_xt[b]], [], dma=True)

    stageA(0)
    for i in range(NT):
        stageB1(i)
        if i + 1 < NT:
            stageA(i + 1)
        stageB2(i)
    S.barrier()
    A.release(m0)


class Psum:
    pass


def setup_common(nc, st, arena_bytes):
    A = Arena(st.enter_context(nc.sbuf_tensor("arena", [128, arena_bytes // 4], F32)))
    P = Psum()
    P.f = []
    P.bf = []
    P.r = []
    for i in range(8):
        t = st.enter_context(nc.psum_tensor("ps%d" % i, [128, 512], F32))
        P.f.append(t[:, :])
        P.bf.append(t[:, :].bitcast(BF16))
        P.r.append(Res("ps%d" % i))
    return A, P


class NormT:
    def __init__(self, S, A, P, K, TT, nbuf_x=2, nbuf_h=2):
        self.S, self.P, self.K, self.TT = S, P, K, TT
        NS = TT // 128
        self.NS = NS
        self.xt = [A.alloc((NS, D), F32) for _ in range(nbuf_x)]
        self.r_xt = [Res() for _ in range(nbuf_x)]
        self.hb = A.alloc((NS, D), BF16)
        self.r_hb = Res()
        self.hT = [A.alloc((NKC, TT), BF16) for _ in range(nbuf_h)]
        self.r_hT = [[Res() for _ in range(NKC)] for _ in range(nbuf_h)]
        self.junk = A.alloc((D,), BF16)
        self.r_junk = Res()
        self.ss = A.alloc((2 * NS,), F32)
        self.r_ss = Res()
        self.rs = A.alloc((2 * NS,), F32)
        self.r_rs = Res()
        self.sd = A.alloc((2 * NS,), F32)
        self.r_sd = Res()
        self.tcnt = 0

    def run(self, i, x_src_tile, g_sb):
        S, P, K, TT, NS = self.S, self.P, self.K, self.TT, self.NS
        b = i % len(self.xt)
        hbuf = i % len(self.hT)
        xt, r_xt = self.xt[b], self.r_xt[b]
        hT, r_hT = self.hT[hbuf], self.r_hT[hbuf]
        ss, rs, sd, hb, junk = self.ss, self.rs, self.sd, self.hb, self.junk
        b2 = i % 2
        src = x_src_tile.rearrange("(s p) d -> p s d", p=128)
        S.add("sp", lambda e: e.dma_start(out=xt, in_=src), [], [r_xt], dma=True)
        for s in range(NS):
            S.add("act", lambda e, s=s: e.activation(out=junk, in_=xt[:, s, :], func=AF.Square,
                                                     accum_out=ss[:, b2 * NS + s:b2 * NS + s + 1]),
                  [r_xt], [self.r_junk, self.r_ss])
        S.add("act", lambda e: e.activation(out=sd[:, b2 * NS:(b2 + 1) * NS], in_=ss[:, b2 * NS:(b2 + 1) * NS],
                                            func=AF.Sqrt, scale=1.0 / D, bias=K.eps_col),
              [self.r_ss, K.r_const], [self.r_sd])
        S.add("dve", lambda e: e.reciprocal(out=rs[:, b2 * NS:(b2 + 1) * NS], in_=sd[:, b2 * NS:(b2 + 1) * NS]),
              [self.r_sd], [self.r_rs])
        for s in range(NS):
            S.add("dve", lambda e, s=s: e.tensor_scalar(out=hb[:, s, :], in0=xt[:, s, :],
                                                        scalar1=rs[:, b2 * NS + s:b2 * NS + s + 1], scalar2=None,
                                                        op0=ALU.mult),
                  [r_xt, self.r_rs], [self.r_hb])
        for kc in range(NKC):
            q = self.tcnt % 2
            self.tcnt += 1
            pT, r_pT = P.bf[q], P.r[q]
            for s in range(NS):
                S.add("pe", lambda e, s=s, kc=kc, pT=pT: e.transpose(out=pT[:, s * 128:(s + 1) * 128],
                                                                    in_=hb[:, s, kc * 128:(kc + 1) * 128],
                                                                    identity=K.ident),
                      [self.r_hb, K.r_const], [r_pT])
            if kc % 2 == 0:
                S.add("dve", lambda e, kc=kc, pT=pT: e.tensor_scalar(out=hT[:, kc, :], in0=pT[:, 0:TT],
                                                                     scalar1=g_sb[:, kc:kc + 1], scalar2=None, op0=ALU.mult),
                      [r_pT, K.r_const], [r_hT[kc]])
            else:
                S.add("act", lambda e, kc=kc, pT=pT: e.activation(out=hT[:, kc, :], in_=pT[:, 0:TT], func=AF.Copy,
                                                                  scale=g_sb[:, kc:kc + 1]),
                      [r_pT, K.r_const], [r_hT[kc]])
        return xt, r_xt, hT, r_hT


IN_COLS = 3072


def win_phase(S, A, P, K, x_src, g_sb, win_d, qg_sb, kg_sb, scr, T, TT=256):
    m0 = A.mark()
    NS = TT // 128
    NT = T // TT
    w = A.alloc((NKC, IN_COLS), BF16)
    r_w = [Res() for _ in range(NKC)]
    jobs = []
    for kc in range(NKC):
        for part in range(2):
            jobs.append((win_d[kc * 128:(kc + 1) * 128, part * 1536:(part + 1) * 1536],
                         w[:, kc, part * 1536:(part + 1) * 1536], r_w[kc], (1536,)))
    load_weight_rows(S, A, jobs)
    S.barrier()
    nt = NormT(S, A, P, K, TT, nbuf_x=1, nbuf_h=2)
    sq = [A.alloc((TT,), BF16) for _ in range(2)]
    r_sq = [Res() for _ in range(2)]
    zf = [A.alloc((TT,), F32) for _ in range(2)]
    r_zf = [Res() for _ in range(2)]
    rstd = [A.alloc((TT,), F32) for _ in range(2)]
    r_rstd = [Res() for _ in range(2)]
    qk_o = [A.alloc((TT,), BF16) for _ in range(3)]
    r_qk_o = [Res() for _ in range(3)]
    pl_o = [A.alloc((TT,), F32) for _ in range(2)]
    r_pl_o = [Res() for _ in range(2)]
    v_o = [A.alloc((1024,), BF16) for _ in range(2)]
    r_v_o = [Res() for _ in range(2)]
    for bi in range(2):
        S.add("pool", lambda e, bi=bi: e.memset(v_o[bi], 1.0), [], [r_v_o[bi]])
    h_o = [A.alloc((1280,), F32) for _ in range(2)]
    r_h_o = [Res() for _ in range(2)]
    cn = {"fm": 0, "qk": 0, "pl": 0, "v": 0, "h": 0, "tm": 0}
    for i in range(NT):
        t0 = i * TT
        xt, r_xt, hT, r_hT = nt.run(i, x_src[t0:t0 + TT, :], g_sb)
        for ci in range(10):
            col0 = ci * 128 if ci < 8 else 1536 + (ci - 8) * 128
            q = cn["fm"] % 2
            cn["fm"] += 1
            pz, r_pz = P.f[2 + q], P.r[2 + q]
            for kc in range(NKC):
                S.add("pe", lambda e, kc=kc, col0=col0, pz=pz, hT=hT: e.matmul(pz[:, 0:TT], lhsT=w[:, kc, col0:col0 + 128],
                                                                      rhs=hT[:, kc, :], start=(kc == 0), stop=(kc == NKC - 1)),
                      [r_w[kc], r_hT[kc]], [r_pz])
            if ci < 8:
                b = cn["qk"] % 2
                b3 = cn["qk"] % 3
                cn["qk"] += 1
                gains = qg_sb if ci < 4 else kg_sb
                gi = ci % 4
                S.add("act", lambda e, pz=pz, b=b: e.activation(out=sq[b], in_=pz[:, 0:TT], func=AF.Square),
                      [r_pz], [r_sq[b]])
                S.add("act", lambda e, pz=pz, b=b: e.copy(zf[b], pz[:, 0:TT]), [r_pz], [r_zf[b]])
                pm, r_pm = P.f[4], P.r[4]
                S.add("pe", lambda e, b=b: e.matmul(pm[:, 0:TT], lhsT=K.bd64, rhs=sq[b], start=True, stop=True),
                      [r_sq[b], K.r_const], [r_pm])
                S.add("act", lambda e, b=b, ci=ci: e.activation(out=rstd[b], in_=pm[:, 0:TT], func=AF.Sqrt,
                                                                scale=(64.0 if ci < 4 else 1.0),
                                                                bias=(K.eps64_col if ci < 4 else K.eps_col)),
                      [r_pm, K.r_const], [r_rstd[b]])
                S.add("dve", lambda e, b=b: e.reciprocal(out=rstd[b], in_=rstd[b]), [r_rstd[b]], [r_rstd[b]])
                S.add("dve", lambda e, b=b, b3=b3, gains=gains, gi=gi: e.scalar_tensor_tensor(
                    out=qk_o[b3], in0=zf[b], scalar=gains[:, gi:gi + 1], in1=rstd[b], op0=ALU.mult, op1=ALU.mult),
                    [r_zf[b], r_rstd[b], K.r_const], [r_qk_o[b3]])
                dst = (scr.qT if ci < 4 else scr.kT)[gi, :, t0:t0 + TT]
                S.add("pool", lambda e, b3=b3, dst=dst: e.dma_start(out=dst, in_=qk_o[b3]), [r_qk_o[b3]], [], dma=True)
            else:
                b = cn["pl"] % 2
                cn["pl"] += 1
                S.add("act", lambda e, pz=pz, b=b: e.copy(pl_o[b], pz[:, 0:TT]), [r_pz], [r_pl_o[b]])
                dst = scr.pT[ci - 8, :, t0:t0 + TT]
                S.add("pool", lambda e, b=b, dst=dst: e.dma_start(out=dst, in_=pl_o[b]), [r_pl_o[b]], [], dma=True)
        for s in range(NS):
            groups = [(1024, 512, "v", 0), (1792, 512, "h", 0), (2304, 512, "h", 512), (2816, 256, "h", 1024)]
            bv = cn["v"] % 2
            cn["v"] += 1
            for (c0, ncol, kind, off) in groups:
                q = cn["tm"] % 3
                cn["tm"] += 1
                pz, r_pz = P.f[5 + q], P.r[5 + q]
                for kc in range(NKC):
                    S.add("pe", lambda e, kc=kc, c0=c0, ncol=ncol, pz=pz, s=s, hT=hT: e.matmul(
                        pz[:, 0:ncol], lhsT=hT[:, kc, s * 128:(s + 1) * 128], rhs=w[:, kc, c0:c0 + ncol],
                        start=(kc == 0), stop=(kc == NKC - 1)),
                        [r_w[kc], r_hT[kc]], [r_pz])
                if kind == "v":
                    S.add("dve", lambda e, pz=pz, bv=bv: e.tensor_copy(
                        v_o[bv].rearrange("p (h c) -> p h c", c=128)[:, :, 0:64], pz[:, 0:512].rearrange("p (h c) -> p h c", c=64)),
                        [r_pz], [r_v_o[bv]])
                else:
                    eng = "act" if off == 512 else "dve"
                    if eng == "act":
                        S.add("act", lambda e, pz=pz, bv=bv, off=off, ncol=ncol: e.copy(h_o[bv][:, off:off + ncol], pz[:, 0:ncol]),
                              [r_pz], [r_h_o[bv]])
                    else:
                        S.add("dve", lambda e, pz=pz, bv=bv, off=off, ncol=ncol: e.tensor_copy(h_o[bv][:, off:off + ncol], pz[:, 0:ncol]),
                              [r_pz], [r_h_o[bv]])
            tok0 = t0 + s * 128
            S.add("pool", lambda e, bv=bv, tok0=tok0: e.dma_start(out=scr.v2[tok0:tok0 + 128, :], in_=v_o[bv]),
                  [r_v_o[bv]], [], dma=True)
            S.add("pool", lambda e, bv=bv, tok0=tok0: e.dma_start(out=scr.h[tok0:tok0 + 128, :], in_=h_o[bv]),
                  [r_h_o[bv]], [], dma=True)
    S.barrier()
    A.release(m0)


def wout_phase(S, A, P, K, x_src, x_dst, wout_d, scr, T, TT=512):
    m0 = A.mark()
    NS = TT // 128
    NT = T // TT
    w = A.alloc((NKC, D), BF16)
    r_w = [Res() for _ in range(NKC // 2)]
    jobs = []
    for k2 in range(NKC // 2):
        src = wout_d[k2 * 256:(k2 + 1) * 256, :].rearrange("(s p) d -> p s d", p=128)
        jobs.append((src, w[:, 2 * k2:2 * k2 + 2, :], r_w[k2], (2, D)))
    load_weight_rows(S, A, jobs)
    S.barrier()
    xt = [A.alloc((NS, D), F32) for _ in range(2)]
    r_xt = [Res() for _ in range(2)]
    yt = [A.alloc((NKC, TT), BF16) for _ in range(2)]
    r_yt = [Res() for _ in range(2)]
    oc = 0
    for i in range(NT):
        b = i % 2
        t0 = i * TT
        src = x_src[t0:t0 + TT, :].rearrange("(s p) d -> p s d", p=128)
        S.add("sp", lambda e, b=b, src=src: e.dma_start(out=xt[b], in_=src), [], [r_xt[b]], dma=True)
        ysrc = scr.yT[:, :, t0:t0 + TT].rearrange("k p t -> p k t")
        S.add("sp", lambda e, b=b, ysrc=ysrc: e.dma_start(out=yt[b], in_=ysrc), [], [r_yt[b]], dma=True)
        for s in range(NS):
            for dh in range(2):
                q = oc % 2
                oc += 1
                po, r_po = P.f[6 + q], P.r[6 + q]
                for kc in range(NKC):
                    S.add("pe", lambda e, kc=kc, s=s, dh=dh, po=po, b=b: e.matmul(
                        po, lhsT=yt[b][:, kc, s * 128:(s + 1) * 128], rhs=w[:, kc, dh * 512:(dh + 1) * 512],
                        start=(kc == 0), stop=(kc == NKC - 1)), [r_yt[b], r_w[kc // 2]], [r_po])
                S.add("dve", lambda e, s=s, dh=dh, po=po, b=b: e.tensor_tensor(
                    out=xt[b][:, s, dh * 512:(dh + 1) * 512], in0=po, in1=xt[b][:, s, dh * 512:(dh + 1) * 512], op=ALU.add),
                    [r_po, r_xt[b]], [r_xt[b]])
        dst = x_dst[t0:t0 + TT, :].rearrange("(s p) d -> p s d", p=128)
        S.add("sp", lambda e, b=b, dst=dst: e.dma_start(out=dst, in_=xt[b]), [r_xt[b]], [], dma=True)
    S.barrier()
    A.release(m0)


DILS = (1, 4, 16)
AT_DBG = set()
NEGM = -30000.0


def attn_phase(S, A, P, K, bt_d, mask_d, scr, T):
    m0 = A.mark()
    BTb = A.alloc((12, 512), BF16)
    r_BT = Res()
    m1 = A.mark()
    BTf = A.alloc((24, 256), F32)
    r_BTf = Res()
    mk = A.alloc((256,), F32)
    r_mk = Res()
    S.add("sp", lambda e: e.dma_start(out=mk, in_=mask_d), [], [r_mk], dma=True)
    for g in range(3):
        src = bt_d[g].rearrange("h p c -> p h c")
        S.add("sp", lambda e, g=g, src=src: e.dma_start(out=BTf[:, g * 8:(g + 1) * 8, :], in_=src), [], [r_BTf], dma=True)
    BTv = BTb.rearrange("p a (h c) -> p (a h) c", h=2)
    for gh in range(24):
        S.add("dve", lambda e, gh=gh: e.tensor_tensor(out=BTf[:, gh, :], in0=BTf[:, gh, :], in1=mk, op=ALU.add),
              [r_BTf, r_mk], [r_BTf])
        S.add("act", lambda e, gh=gh: e.activation(out=BTv[:, gh, :], in_=BTf[:, gh, :], func=AF.Exp), [r_BTf], [r_BT])
    S.barrier()
    A.release(m1)
    NB = 4
    qT = A.alloc((T,), BF16)
    kT = A.alloc((T,), BF16)
    r_qk = Res()
    vt = [A.alloc((T // 128, 256), BF16) for _ in range(2)]
    r_vt = [Res() for _ in range(2)]
    acc = A.alloc((2, T), F32)
    r_acc = Res()
    yo = A.alloc((T,), BF16, parts=64)
    r_yo = Res()
    rden = [A.alloc((512,), F32, parts=64) for _ in range(2)]
    r_rden = [Res() for _ in range(2)]
    E = [A.alloc((512,), BF16) for _ in range(NB)]
    r_E = [Res() for _ in range(NB)]
    E0 = [A.alloc((512,), BF16) for _ in range(NB)]
    r_E0 = [Res() for _ in range(NB)]
    cn = 0
    vcnt = 0
    for hp in range(4):
        S.add("sp", lambda e, hp=hp: e.dma_start(out=qT, in_=scr.qT[hp]), [], [r_qk], dma=True)
        S.add("sp", lambda e, hp=hp: e.dma_start(out=kT, in_=scr.kT[hp]), [], [r_qk], dma=True)
        if "nomemset" not in AT_DBG:
            S.add("pool", lambda e: e.memset(acc, 0.0), [], [r_acc])
        for g, d in enumerate(DILS):
            L = T // d
            nl = L // 128
            vb = vcnt % 2
            vcnt += 1
            vtg, r_vtg = vt[vb], r_vt[vb]
            vsrc = scr.v2[:, hp * 256:(hp + 1) * 256].rearrange("(n p r) c -> p r n c", p=128, r=d)
            vdst = vtg.rearrange("p (r n) c -> p r n c", r=d)
            for r in range(d):
                for n0 in range(0, nl, 8):
                    if "novt" in AT_DBG:
                        continue
                    n1 = min(nl, n0 + 8)
                    S.add("sp", lambda e, r=r, n0=n0, n1=n1, vsrc=vsrc, vdst=vdst: e.dma_start(
                        out=vdst[:, r, n0:n1, :], in_=vsrc[:, r, n0:n1, :]), [], [r_vtg], dma=True)
            btp = BTb[:, g * 4 + hp, :]
            btv = btp.rearrange("p (h c) -> p h c", h=2)
            pend = []
            LA = 2

            def back(item):
                (po, r_po, b, hw, nq, kts, r_, av) = item
                nk = len(kts)
                for hh in range(2):
                    for j, kt in enumerate(kts):
                        c0 = hh * hw + j * nq
                        S.add("pe", lambda e, po=po, j=j, kt=kt, b=b, r_=r_, nl=nl, hh=hh, nq=nq, nk=nk, c0=c0, vtg=vtg: e.matmul(
                            po[:, hh * 128:hh * 128 + nq], lhsT=vtg[:, r_ * nl + kt, hh * 128:(hh + 1) * 128],
                            rhs=E[b][:, c0:c0 + nq], start=(j == 0), stop=(j == nk - 1)),
                            [r_vtg, r_E[b]], [r_po])
                pv = po[:, 0:256].rearrange("p (a b) -> p a b", a=2)[:, :, 0:nq]
                S.add("dve", lambda e, av=av, pv=pv: e.tensor_tensor(out=av, in0=pv, in1=av, op=ALU.add),
                      [r_po, r_acc], [r_acc])

            for r in range(d):
                def cols(q0, n, r=r, d=d):
                    a = r + q0 * d
                    return slice(a, a + (n - 1) * d + 1, d)
                steps = [("first", 0, 64, [0])]
                for m in range(nl - 1):
                    steps.append(("main", 64 + 128 * m, 128, [m, m + 1]))
                steps.append(("last", L - 64, 64, [nl - 1]))
                for (kind, q0, nq, kts) in steps:
                    b = cn % NB
                    b3 = cn % 3
                    bo = cn % 2
                    cn += 1
                    psh = [P.f[2 * b3], P.f[2 * b3 + 1]]
                    r_psh = [P.r[2 * b3], P.r[2 * b3 + 1]]
                    po, r_po = P.f[6 + bo], P.r[6 + bo]
                    if kind == "main":
                        bsl = btp
                        hw = 256
                    else:
                        bsl = btv[:, :, 192:256] if kind == "first" else btv[:, :, 0:64]
                        hw = 64
                    for hh in range(2):
                        pr = slice(hh * 64, (hh + 1) * 64)
                        qs = qT[pr, cols(q0, nq)]
                        for j, kt in enumerate(kts):
                            ksl = kT[pr, cols(kt * 128, 128)]
                            c0 = j * nq
                            S.add("pe", lambda e, ps=psh[hh], ksl=ksl, qs=qs, c0=c0, nq=nq: e.matmul(
                                ps[:, c0:c0 + nq], lhsT=ksl, rhs=qs, start=True, stop=True), [r_qk], [r_psh[hh]])
                    for hh in range(2):
                        S.add("act", lambda e, ps=psh[hh], b=b, hw=hw, hh=hh: e.activation(out=E0[b][:, hh * hw:(hh + 1) * hw], in_=ps[:, 0:hw],
                                                                                   func=AF.Exp), [r_psh[hh]], [r_E0[b]])
                    if kind == "main":
                        S.add("pool", lambda e, b=b, bsl=bsl: e.tensor_tensor(out=E[b], in0=E0[b], in1=bsl, op=ALU.mult),
                              [r_E0[b], r_BT], [r_E[b]])
                    else:
                        S.add("pool", lambda e, b=b, bsl=bsl: e.tensor_tensor(
                            out=E[b][:, 0:128].rearrange("p (h c) -> p h c", h=2), in0=E0[b][:, 0:128].rearrange("p (h c) -> p h c", h=2),
                            in1=bsl, op=ALU.mult), [r_E0[b], r_BT], [r_E[b]])
                    pend.append((po, r_po, b, hw, nq, kts, r, acc[:, :, cols(q0, nq)]))
                    if len(pend) > LA:
                        back(pend.pop(0))
            while pend:
                back(pend.pop(0))
        for hh in range(2):
            for ci in range(T // 512):
                if "nofin" in AT_DBG:
                    continue
                b = cn % 4
                cn += 1
                ps, r_ps = P.f[b], P.r[b]
                cs = slice(ci * 512, (ci + 1) * 512)
                S.add("pe", lambda e, ps=ps, hh=hh, cs=cs: e.matmul(ps[0:64, :], lhsT=K.shift_f, rhs=acc[:, hh, cs], start=True, stop=True),
                      [r_acc, K.r_const], [r_ps])
                rb = ci % 2
                S.add("dve", lambda e, ps=ps, rb=rb: e.reciprocal(out=rden[rb], in_=ps[0:64, :]), [r_ps], [r_rden[rb]])
                S.add("dve", lambda e, hh=hh, cs=cs, rb=rb: e.tensor_tensor(out=yo[:, cs], in0=acc[0:64, hh, cs], in1=rden[rb], op=ALU.mult),
                      [r_acc, r_rden[rb]], [r_yo])
            S.add("sp", lambda e, hp=hp, hh=hh: e.dma_start(out=scr.yT[hp, hh * 64:(hh + 1) * 64, :], in_=yo),
                  [r_yo], [], dma=True)
    S.barrier()
    A.release(m0)


POOL_WINDOWS = (2, 4, 8, 16)


def pool_phase(S, A, P, K, pw_d, psc_sb, pinv_d, scr, T):
    m0 = A.mark()
    PADL = 16
    W = T + 32
    bufs = [A.alloc((W,), F32) for _ in range(3)]
    r_b = [Res() for _ in range(3)]
    R = A.alloc((T,), BF16)
    r_R = Res()
    yo = A.alloc((T,), BF16)
    r_yo = Res()
    wf = A.alloc((128,), F32)
    wb = A.alloc((128,), BF16)
    r_w = Res()
    pinv = A.alloc((16,), F32)
    r_pinv = Res()
    tmp = A.alloc((16,), F32)
    r_tmp = Res()
    for half in range(2):
        S.add("pool", lambda e: e.memset(wf, 0.0), [], [r_w])
        for j in range(2):
            S.add("sp", lambda e, j=j, half=half: e.dma_start(out=wf[j * 64:(j + 1) * 64, j * 64:(j + 1) * 64],
                                                             in_=pw_d[half * 2 + j]), [], [r_w], dma=True)
        S.add("dve", lambda e: e.tensor_copy(wb, wf), [r_w], [r_w])
        S.add("sp", lambda e, half=half: e.dma_start(out=pinv, in_=pinv_d[half]), [], [r_pinv], dma=True)
        for bi in range(3):
            S.add("pool", lambda e, bi=bi: e.memset(bufs[bi], 0.0), [], [r_b[bi]])
        S.add("sp", lambda e, half=half: e.dma_start(out=bufs[0][:, PADL:PADL + T], in_=scr.pT[half]), [], [r_b[0]], dma=True)
        for j in range(2):
            w = POOL_WINDOWS[half * 2 + j]
            pr = slice(j * 64, (j + 1) * 64)
            eng = "dve" if j == 0 else "pool"
            cur = 1
            S.add(eng, lambda e, pr=pr: e.tensor_tensor(out=bufs[1][pr, 1:W], in0=bufs[0][pr, 0:W - 1], in1=bufs[0][pr, 1:W], op=ALU.add),
                  [r_b[0]], [r_b[1]])
            sh = 1
            lo, hi = 1, W
            while 2 * sh < w:
                nxt = 2 if cur == 1 else 1
                lo2, hi2 = lo + sh, hi - sh
                S.add(eng, lambda e, pr=pr, cur=cur, nxt=nxt, lo2=lo2, hi2=hi2, sh=sh: e.tensor_tensor(
                    out=bufs[nxt][pr, lo2:hi2], in0=bufs[cur][pr, lo2 - sh:hi2 - sh], in1=bufs[cur][pr, lo2 + sh:hi2 + sh], op=ALU.add),
                    [r_b[cur]], [r_b[nxt]])
                cur = nxt
                lo, hi = lo2, hi2
                sh *= 2
            ww = bufs[cur][pr, PADL:PADL + T]
            uu = bufs[0][pr, PADL:PADL + T]
            S.add("dve", lambda e, pr=pr, ww=ww, uu=uu, w=w: e.scalar_tensor_tensor(out=R[pr, :], in0=ww, scalar=1.0 / w, in1=uu,
                                                                              op0=ALU.mult, op1=ALU.subtract),
                  [r_b[cur], r_b[0]], [r_R])
            for (c0, p0) in ((0, 0), (T - 8, 8)):
                S.add(eng, lambda e, pr=pr, ww=ww, c0=c0, p0=p0: e.tensor_tensor(out=tmp[pr, p0:p0 + 8], in0=ww[:, c0:c0 + 8],
                                                                             in1=pinv[pr, p0:p0 + 8], op=ALU.mult),
                      [r_b[cur], r_pinv], [r_tmp])
                S.add(eng, lambda e, pr=pr, uu=uu, c0=c0, p0=p0: e.tensor_tensor(out=R[pr, c0:c0 + 8], in0=tmp[pr, p0:p0 + 8],
                                                                             in1=uu[:, c0:c0 + 8], op=ALU.subtract),
                      [r_tmp, r_b[0], r_R], [r_R])
        for ti in range(T // 512):
            q = ti % 2
            ps, r_ps = P.f[q], P.r[q]
            S.add("pe", lambda e, ps=ps, ti=ti: e.matmul(ps, lhsT=wb, rhs=R[:, ti * 512:(ti + 1) * 512], start=True, stop=True),
                  [r_w, r_R], [r_ps])
            S.add("act", lambda e, ps=ps, ti=ti, half=half: e.activation(out=yo[:, ti * 512:(ti + 1) * 512], in_=ps, func=AF.Copy,
                                                                      scale=psc_sb[:, half:half + 1]),
                  [r_ps, K.r_const], [r_yo])
        S.add("sp", lambda e, half=half: e.dma_start(out=scr.yT[4 + half], in_=yo), [r_yo], [], dma=True)
    S.barrier()
    A.release(m0)


HG_SKIP = set()


def hgrn_phase(S, A, P, K, layer, lbl_d, gn_sb, hc_d, scr, T):
    m0 = A.mark()
    NTL = T // 128
    C = 32
    tri = A.alloc((2, 128), F32)
    sel = A.alloc((2, 4), F32)
    msk = A.alloc((2, 512), F32)
    rmask = A.alloc((4,), F32)
    r_c = Res()
    S.add("sp", lambda e: e.dma_start(out=rmask, in_=hc_d["rmask"]), [], [r_c], dma=True)
    S.add("sp", lambda e: e.dma_start(out=tri, in_=hc_d["tri"].rearrange("d p c -> p d c")), [], [r_c], dma=True)
    S.add("sp", lambda e: e.dma_start(out=sel, in_=hc_d["sel"].rearrange("d p c -> p d c")), [], [r_c], dma=True)
    S.add("sp", lambda e: e.dma_start(out=msk, in_=hc_d["mask"].rearrange("d p c -> p d c")), [], [r_c], dma=True)
    oml = A.alloc((2, 256), F32)
    r_oml = Res()
    if layer == 0:
        S.add("pool", lambda e: e.memset(oml, 1.0), [], [r_oml])
    else:
        lrow = A.alloc((2, 2, 256), F32, parts=1)
        drow = A.alloc((2, 256), F32, parts=1)
        r_l = Res()
        S.add("sp", lambda e: e.dma_start(out=lrow, in_=lbl_d.rearrange("(o a) b c -> o a b c", o=1)), [], [r_l], dma=True)
        S.add("dve", lambda e: e.tensor_tensor(out=drow, in0=lrow[:, :, 1, :], in1=lrow[:, :, 0, :], op=ALU.subtract), [r_l], [r_l])
        pb, r_pb = P.f[0], P.r[0]
        S.add("pe", lambda e: e.matmul(pb, lhsT=K.ones_f[0:1, :], rhs=drow.rearrange("p a b -> p (a b)"), start=True, stop=True),
              [r_l, K.r_const], [r_pb])
        S.add("act", lambda e: e.activation(out=oml.rearrange("p a b -> p (a b)"), in_=pb, func=AF.Sigmoid, scale=-1.0),
              [r_pb], [r_oml])
    nb = 2
    hz = [A.alloc((1280,), F32) for _ in range(nb)]
    r_hz = [Res() for _ in range(nb)]

    def mk(shape, dt, parts=128):
        return [A.alloc(shape, dt, parts) for _ in range(nb)], [Res() for _ in range(nb)]
    sgm, r_sgm = mk((256,), F32)
    kk, r_kk = mk((256,), F32)
    ff, r_ff = mk((256,), F32)
    G, r_G = mk((256,), F32)
    EBm, r_EBm = mk((256,), F32)
    EBp, r_EBp = mk((256,), F32)
    Kt, r_Kt = mk((256,), BF16)
    KtM, r_KtM = mk((4, 256), BF16)
    Qt, r_Qt = mk((256,), BF16)
    Vb, r_Vb = mk((256,), BF16)
    QT, r_QT = mk((4, 128), BF16, 64)
    KT, r_KT = mk((4, 128), BF16, 64)
    a_sb, r_a = mk((4, 4), F32, 64)
    attm, r_attm = mk((512,), BF16)
    Sbf, r_Sbf = mk((4, 256), BF16, 64)
    tmp, r_tmp = mk((256,), F32, 64)
    osb, r_osb = mk((512,), F32, 64)
    obl, r_obl = mk((512,), F32, 64)
    sqo, r_sqo = mk((512,), F32, 64)
    rst, r_rst = mk((512,), F32, 64)
    sgt, r_sgt = mk((256,), BF16)
    t1, r_t1 = mk((512,), F32, 64)
    yo, r_yo = mk((4, 128), BF16, 64)
    St = A.alloc((256,), F32, parts=64)
    r_St = Res()
    pB, r_pB = P.f[0], P.r[0]
    pq, r_pq = P.bf[1], P.r[1]
    pk, r_pk = P.bf[2], P.r[2]
    pu = [P.f[3], P.f[4]]
    r_pu = [P.r[3], P.r[4]]
    pa2, r_pa2 = P.f[5], P.r[5]
    po, r_po = P.f[6], P.r[6]
    pms, r_pms = P.f[7], P.r[7]
    for (dd, order) in ((1, list(range(NTL - 1, -1, -1))), (0, list(range(NTL)))):
        S.add("pool", lambda e: e.memset(St, 0.0), [], [r_St])
        for it, ti in enumerate(order):
            b = it % nb
            tok0 = ti * 128
            S.add("sp", lambda e, b=b, tok0=tok0: e.dma_start(out=hz[b], in_=scr.h[tok0:tok0 + 128, :]), [], [r_hz[b]], dma=True)
            hq = hz[b][:, 0:256]
            hi = hz[b][:, 256:512]
            hf = hz[b][:, 512 + 256 * dd:768 + 256 * dd]
            hg = hz[b][:, 1024:1280]
            S.add("act", lambda e, b=b, hf=hf: e.activation(out=sgm[b], in_=hf, func=AF.Sigmoid, scale=-1.0), [r_hz[b]], [r_sgm[b]])
            S.add("dve", lambda e, b=b, dd=dd: e.tensor_tensor(out=kk[b], in0=sgm[b], in1=oml[:, dd, :], op=ALU.mult),
                  [r_sgm[b], r_oml], [r_kk[b]])
            S.add("dve", lambda e, b=b: e.tensor_scalar(out=ff[b], in0=kk[b], scalar1=-1.0, scalar2=1.0, op0=ALU.mult, op1=ALU.add),
                  [r_kk[b]], [r_ff[b]])
            S.add("act", lambda e, b=b: e.activation(out=G[b], in_=ff[b], func=AF.Ln), [r_ff[b]], [r_G[b]])
            if "tri" not in HG_SKIP:
              S.add("pe", lambda e, b=b, dd=dd: e.matmul(pB[:, 0:256], lhsT=tri[:, dd, :], rhs=G[b], start=True, stop=True),
                  [r_G[b], r_c], [r_pB])
            S.add("act", lambda e, b=b: e.activation(out=EBm[b], in_=pB[:, 0:256], func=AF.Exp, scale=-1.0), [r_pB], [r_EBm[b]])
            S.add("act", lambda e, b=b: e.activation(out=EBp[b], in_=pB[:, 0:256], func=AF.Exp), [r_pB], [r_EBp[b]])
            S.add("dve", lambda e, b=b: e.tensor_tensor(out=Kt[b], in0=kk[b], in1=EBm[b], op=ALU.mult), [r_kk[b], r_EBm[b]], [r_Kt[b]])
            S.add("pool", lambda e, b=b, hq=hq: e.tensor_tensor(out=Qt[b], in0=hq, in1=EBp[b], op=ALU.mult), [r_hz[b], r_EBp[b]], [r_Qt[b]])
            S.add("pool", lambda e, b=b, hi=hi: e.tensor_copy(Vb[b], hi), [r_hz[b]], [r_Vb[b]])
            for h in range(4):
                if "tr" in HG_SKIP:
                    continue
                S.add("pe", lambda e, b=b, h=h: e.transpose(out=pq[0:64, h * 128:(h + 1) * 128], in_=Qt[b][:, h * 64:(h + 1) * 64],
                                                            identity=K.ident), [r_Qt[b], K.r_const], [r_pq])
            for h in range(4):
                if "tr" in HG_SKIP:
                    continue
                S.add("pe", lambda e, b=b, h=h: e.transpose(out=pk[0:64, h * 128:(h + 1) * 128], in_=Kt[b][:, h * 64:(h + 1) * 64],
                                                            identity=K.ident), [r_Kt[b], K.r_const], [r_pk])
            S.add("act", lambda e, b=b: e.copy(QT[b].rearrange("p a b -> p (a b)"), pq[0:64, 0:512]), [r_pq], [r_QT[b]])
            S.add("dve", lambda e, b=b: e.tensor_copy(KT[b].rearrange("p a b -> p (a b)"), pk[0:64, 0:512]), [r_pk], [r_KT[b]])
            for h in range(4):
                if "a" in HG_SKIP:
                    continue
                S.add("pe", lambda e, b=b, h=h, dd=dd: e.matmul(pB[0:64, 256 + h * 4:256 + (h + 1) * 4], lhsT=EBp[b][:, h * 64:(h + 1) * 64],
                                                               rhs=sel[:, dd, :], start=True, stop=True), [r_EBp[b], r_c], [r_pB])
            S.add("dve", lambda e, b=b: e.tensor_copy(a_sb[b].rearrange("p a b -> p (a b)"), pB[0:64, 256:272]), [r_pB], [r_a[b]])
            for j in range(4):
                S.add("pool", lambda e, b=b, j=j: e.tensor_scalar(out=KtM[b][:, j, :], in0=Kt[b], scalar1=rmask[:, j:j + 1], scalar2=None,
                                                                  op0=ALU.mult), [r_Kt[b], r_c], [r_KtM[b]])
            for j in range(4):
                for h in range(4):
                    if "u" in HG_SKIP:
                        continue
                    S.add("pe", lambda e, b=b, j=j, h=h: e.matmul(
                        pu[j // 2][0:64, ((j % 2) * 4 + h) * 64:((j % 2) * 4 + h + 1) * 64],
                        lhsT=KtM[b][:, j, h * 64:(h + 1) * 64], rhs=Vb[b][:, h * 64:(h + 1) * 64],
                        start=True, stop=True), [r_KtM[b], r_Vb[b]], [r_pu[j // 2]])
            jorder = (0, 1, 2, 3) if dd == 0 else (3, 2, 1, 0)
            for j in jorder:
                S.add("act", lambda e, b=b, j=j: e.copy(Sbf[b][:, j, :], St), [r_St], [r_Sbf[b]])
                S.add("dve", lambda e, b=b, j=j: e.tensor_tensor(out=tmp[b], in0=pu[j // 2][0:64, (j % 2) * 256:(j % 2 + 1) * 256], in1=St,
                                                                 op=ALU.add), [r_pu[j // 2], r_St], [r_tmp[b]])
                for h in range(4):
                    S.add("dve", lambda e, b=b, j=j, h=h: e.tensor_scalar(out=St[:, h * 64:(h + 1) * 64], in0=tmp[b][:, h * 64:(h + 1) * 64],
                                                                          scalar1=a_sb[b][:, h, j:j + 1], scalar2=None, op0=ALU.mult),
                          [r_tmp[b], r_a[b]], [r_St])
            for h in range(4):
                S.add("pe", lambda e, b=b, h=h: e.matmul(pa2[:, h * 128:(h + 1) * 128], lhsT=KT[b][:, h, :], rhs=QT[b][:, h, :],
                                                         start=True, stop=True), [r_KT[b], r_QT[b]], [r_pa2])
            S.add("dve", lambda e, b=b, dd=dd: e.tensor_tensor(out=attm[b], in0=pa2, in1=msk[:, dd, :], op=ALU.mult),
                  [r_pa2, r_c], [r_attm[b]])
            for h in range(4):
                S.add("pe", lambda e, b=b, h=h: e.matmul(po[0:64, h * 128:(h + 1) * 128], lhsT=Vb[b][:, h * 64:(h + 1) * 64],
                                                         rhs=attm[b][:, h * 128:(h + 1) * 128], start=True, stop=False),
                      [r_Vb[b], r_attm[b]], [r_po])
                for j in range(4):
                    S.add("pe", lambda e, b=b, h=h, j=j: e.matmul(po[0:64, h * 128 + 32 * j:h * 128 + 32 * j + 32],
                                                                 lhsT=Sbf[b][:, j, h * 64:(h + 1) * 64], rhs=QT[b][:, h, 32 * j:32 * j + 32],
                                                                 start=False, stop=(j == 3)), [r_Sbf[b], r_QT[b]], [r_po])
            obv = scr.ob[:, :, tok0:tok0 + 128]
            if dd == 1:
                S.add("act", lambda e, b=b: e.copy(osb[b], po[0:64, :]), [r_po], [r_osb[b]])
                S.add("pool", lambda e, b=b, obv=obv: e.dma_start(out=obv, in_=osb[b].rearrange("p (a b) -> p a b", a=4)),
                      [r_osb[b]], [], dma=True)
            else:
                S.add("sp", lambda e, b=b, obv=obv: e.dma_start(out=obl[b].rearrange("p (a b) -> p a b", a=4), in_=obv), [], [r_obl[b]], dma=True)
                S.add("dve", lambda e, b=b: e.tensor_tensor(out=osb[b], in0=po[0:64, :], in1=obl[b], op=ALU.add), [r_po, r_obl[b]], [r_osb[b]])
                S.add("act", lambda e, b=b: e.activation(out=sqo[b], in_=osb[b], func=AF.Square), [r_osb[b]], [r_sqo[b]])
                S.add("pe", lambda e, b=b: e.matmul(pms[0:64, :], lhsT=K.mean64_f, rhs=sqo[b], start=True, stop=True),
                      [r_sqo[b], K.r_const], [r_pms])
                S.add("act", lambda e, b=b: e.activation(out=rst[b], in_=pms[0:64, :], func=AF.Sqrt, scale=1.0, bias=K.eps_col[0:64, :]),
                      [r_pms, K.r_const], [r_rst[b]])
                S.add("dve", lambda e, b=b: e.reciprocal(out=rst[b], in_=rst[b]), [r_rst[b]], [r_rst[b]])
                S.add("act", lambda e, b=b, hg=hg: e.activation(out=sgt[b], in_=hg, func=AF.Silu), [r_hz[b]], [r_sgt[b]])
                for h in range(4):
                    S.add("pe", lambda e, b=b, h=h: e.transpose(out=pq[0:64, 512 + h * 128:512 + (h + 1) * 128], in_=sgt[b][:, h * 64:(h + 1) * 64],
                                                                identity=K.ident), [r_sgt[b], K.r_const], [r_pq])
                S.add("dve", lambda e, b=b: e.scalar_tensor_tensor(out=t1[b], in0=osb[b], scalar=gn_sb[:, 0:1], in1=rst[b],
                                                                   op0=ALU.mult, op1=ALU.mult), [r_osb[b], r_rst[b], K.r_const], [r_t1[b]])
                S.add("dve", lambda e, b=b: e.tensor_tensor(out=yo[b].rearrange("p a b -> p (a b)"), in0=t1[b], in1=pq[0:64, 512:1024], op=ALU.mult),
                      [r_t1[b], r_pq], [r_yo[b]])
                for h in range(4):
                    S.add("pool", lambda e, b=b, h=h, tok0=tok0: e.dma_start(
                        out=scr.yT[6 + h // 2, (h % 2) * 64:(h % 2) * 64 + 64, tok0:tok0 + 128], in_=yo[b][:, h, :]),
                        [r_yo[b]], [], dma=True)
        S.barrier()
    A.release(m0)


DEPTH = 2
NUM_BUCKETS = 32
MAX_DISTANCE = 1024


def _t5_bucket_np(rel):
    half = NUM_BUCKETS // 2
    max_exact = half // 2
    base = np.where(rel > 0, half, 0)
    n = np.abs(rel)
    nf = np.maximum(n, 1).astype(np.float32)
    large = max_exact + (np.log(nf / np.float32(max_exact)) / np.float32(np.log(MAX_DISTANCE / max_exact))
                         * np.float32(half - max_exact)).astype(np.int32)
    large = np.minimum(large, half - 1)
    return base + np.where(n < max_exact, n, large)


def _attn_rel():
    j = np.arange(128)[:, None]
    c = np.arange(256)[None, :]
    rel = np.where(c < 128, j - c - 64, j - (c - 128) + 64)
    return rel


def host_constants(T):
    rel = _attn_rel()
    valid = np.abs(rel) <= 64
    c = {}
    c["amask"] = np.where(valid, 0.0, NEGM).astype(np.float32)
    c["abkt"] = np.stack([np.where(valid, _t5_bucket_np(rel * d), 0) for d in DILS])
    c["avalid"] = valid
    mats = np.zeros((128, 128 * 3 + 64 + 128), np.float32)
    mats[:, 0:128] = np.eye(128)
    bd = np.zeros((128, 128), np.float32)
    bd[0:64, 0:64] = 1.0 / 64
    bd[64:128, 64:128] = 1.0 / 64
    mats[:, 128:256] = bd
    mats[:, 256:384] = 1.0
    mats[0:64, 384:448] = 1.0 / 64
    for i in range(64):
        mats[64 + i, 448 + i] = 1.0
    c["mats"] = mats
    s = np.arange(128)[:, None]
    t = np.arange(128)[None, :]
    same = (s // 32) == (t // 32)
    tri = np.stack([(same & (s <= t)), (same & (s >= t))]).astype(np.float32)
    sel = np.zeros((2, 128, 4), np.float32)
    for j in range(4):
        sel[0, 32 * j + 31, j] = 1.0
        sel[1, 32 * j, j] = 1.0
    c["h_tri"] = tri
    c["h_sel"] = sel
    c["h_rmask"] = ((np.arange(128)[:, None] // 32) == np.arange(4)[None, :]).astype(np.float32)
    c["h_mask"] = np.ascontiguousarray(np.tile(tri, (1, 1, 4)))
    pinv = np.zeros((2, 128, 16), np.float32)
    for half in range(2):
        for p in range(128):
            w = POOL_WINDOWS[half * 2 + p // 64]
            for k in range(8):
                for (pos, col) in ((k, k), (T - 8 + k, 8 + k)):
                    lo = max(pos - w // 2, 0)
                    hi = min(pos + w - w // 2, T)
                    pinv[half, p, col] = 1.0 / (hi - lo)
    c["pinv"] = pinv
    return c


NVEC = 6 * 8 + 2 * 4 + 2 * 4 + 2 * 2 + 2


def layout_vecs(inp):
    v = np.zeros((128, NVEC), np.float32)
    o = 0
    for name in ("ffn1_norm", "mix_norm", "ffn2_norm"):
        for l in range(DEPTH):
            v[:, o:o + 8] = np.asarray(inp[name][l], np.float32).reshape(8, 128).T
            o += 8
    for name in ("q_norm", "k_norm"):
        for l in range(DEPTH):
            v[:, o:o + 4] = np.asarray(inp[name][l], np.float32).reshape(4, 128).T
            o += 4
    for l in range(DEPTH):
        v[:, o:o + 2] = np.asarray(inp["pool_scale"][l], np.float32).reshape(2, 128).T
        o += 2
    for l in range(DEPTH):
        v[0:64, o] = np.asarray(inp["hgrn_norm"][l], np.float32)
        o += 1
    return v


def build_program(T):
    import contextlib
    nc = bass.Bass("TRN2", target_bir_lowering=False)
    dt = lambda name, shape, dtype=F32, kind="ExternalInput": nc.dram_tensor(name, list(shape), dtype, kind=kind).ap()
    x_d = dt("x", [T, D])
    vec_d = dt("vecs", [128, NVEC])
    mats_d = dt("mats", [128, 576])
    w = {}
    for nm in ("ffn1", "ffn2"):
        w[nm + "_w_gate"] = dt(nm + "_w_gate", [DEPTH, D, DFF])
        w[nm + "_w_up"] = dt(nm + "_w_up", [DEPTH, D, DFF])
        w[nm + "_w_down"] = dt(nm + "_w_down", [DEPTH, DFF, D])
    w["w_in"] = dt("w_in", [DEPTH, D, IN_COLS])
    w["w_out"] = dt("w_out", [DEPTH, D, D])
    w["pool_w"] = dt("pool_w", [DEPTH, 4, 64, 64])
    lbl_d = dt("hgrn_lb_logits", [2, DEPTH, 256])
    abias_d = dt("abias", [3, 8, 128, 256])
    amask_d = dt("amask", [128, 256])
    pinv_d = dt("pinv", [2, 128, 16])
    hc = {"tri": dt("h_tri", [2, 128, 128]), "sel": dt("h_sel", [2, 128, 4]), "mask": dt("h_mask", [2, 128, 512]),
          "rmask": dt("h_rmask", [128, 4])}
    out_d = dt("out", [T, D], F32, "ExternalOutput")
    scr = Ctx()
    IK = "ExternalOutput" if DEBUG_DUMP else "Internal"
    scr.xa = dt("s_xa", [T, D], F32, IK)
    scr.qT = dt("s_qT", [4, 128, T], BF16, IK)
    scr.kT = dt("s_kT", [4, 128, T], BF16, IK)
    scr.v2 = dt("s_v", [T, 1024], BF16, IK)
    scr.pT = dt("s_pT", [2, 128, T], F32, IK)
    scr.h = dt("s_h", [T, 1280], F32, IK)
    scr.ob = dt("s_ob", [64, 4, T], F32, IK)
    scr.yT = dt("s_yT", [8, 128, T], BF16, IK)
    with contextlib.ExitStack() as st:
        A, P = setup_common(nc, st, 207 * 1024)
        S = Sched(nc)
        K = Ctx()
        K.r_const = Res("const")
        vecs = A.alloc((NVEC,), F32)
        matf = A.alloc((576,), F32)
        r_mf = Res()
        S.add("sp", lambda e: e.dma_start(out=vecs, in_=vec_d), [], [K.r_const], dma=True)
        S.add("sp", lambda e: e.dma_start(out=matf, in_=mats_d), [], [r_mf], dma=True)
        K.ident = A.alloc((128,), BF16)
        K.bd64 = A.alloc((128,), BF16)
        K.ones_bf = A.alloc((128,), BF16)
        S.add("dve", lambda e: e.tensor_copy(K.ident, matf[:, 0:128]), [r_mf], [K.r_const])
        S.add("dve", lambda e: e.tensor_copy(K.bd64, matf[:, 128:256]), [r_mf], [K.r_const])
        S.add("dve", lambda e: e.tensor_copy(K.ones_bf, matf[:, 256:384]), [r_mf], [K.r_const])
        K.ones_f = matf[:, 256:384]
        K.mean64_f = matf[0:64, 384:448]
        K.shift_f = matf[:, 448:512]
        K.eps_col = A.alloc((1,), F32)
        K.eps64_col = A.alloc((1,), F32)
        S.add("pool", lambda e: e.memset(K.eps_col, EPS), [], [K.r_const])
        S.add("pool", lambda e: e.memset(K.eps64_col, 64.0 * EPS), [], [K.r_const])
        S.add("dve", lambda e: e.tensor_copy(K.eps_col, K.eps_col), [r_mf, K.r_const], [K.r_const])
        S.barrier()

        def vcol(o, n):
            return vecs[:, o:o + n]
        for l in range(DEPTH):
            g1 = vcol(0 + 8 * l, 8)
            gm = vcol(16 + 8 * l, 8)
            g2 = vcol(32 + 8 * l, 8)
            qg = vcol(48 + 4 * l, 4)
            kg = vcol(56 + 4 * l, 4)
            psc = vcol(64 + 2 * l, 2)
            gn = vecs[0:64, 68 + l:69 + l]
            src = x_d if l == 0 else scr.xa
            dst = out_d if l == DEPTH - 1 else scr.xa
            steps = [
                lambda: ffn_phase(S, A, P, K, src, scr.xa, g1, w["ffn1_w_gate"][l], w["ffn1_w_up"][l], w["ffn1_w_down"][l], T),
                lambda: win_phase(S, A, P, K, (x_d if DEBUG_WIN_X else scr.xa), gm, w["w_in"][l], qg, kg, scr, T),
                lambda: attn_phase(S, A, P, K, abias_d, amask_d, scr, T),
                lambda: pool_phase(S, A, P, K, w["pool_w"][l], psc, pinv_d, scr, T),
                lambda: hgrn_phase(S, A, P, K, l, lbl_d, gn, hc, scr, T),
                lambda: wout_phase(S, A, P, K, scr.xa, scr.xa, w["w_out"][l], scr, T),
                lambda: ffn_phase(S, A, P, K, scr.xa, dst, g2, w["ffn2_w_gate"][l], w["ffn2_w_up"][l], w["ffn2_w_down"][l], T),
            ]
            for si, stp in enumerate(steps):
                if DEBUG_PHASES is None or (l * 7 + si) in DEBUG_PHASES:
                    stp()
        S.emit()
    return nc, S


_WNAMES = ("ffn1_w_gate", "ffn1_w_up", "ffn1_w_down", "ffn2_w_gate", "ffn2_w_up", "ffn2_w_down",
           "w_in", "w_out", "pool_w", "hgrn_lb_logits")


def make_in_maps(inputs, T):
    c = host_constants(T)
    rb = np.asarray(inputs["rel_bias"], np.float32)
    ab = np.zeros((3, 8, 128, 256), np.float32)
    for g in range(3):
        gathered = rb[c["abkt"][g]]
        gathered = np.where(c["avalid"][:, :, None], gathered, np.float32(0.0))
        ab[g] = gathered.transpose(2, 0, 1)
    common = {"vecs": layout_vecs(inputs), "mats": c["mats"], "abias": ab, "amask": c["amask"], "pinv": c["pinv"],
              "h_tri": c["h_tri"], "h_sel": c["h_sel"], "h_mask": c["h_mask"], "h_rmask": c["h_rmask"]}
    for nm in _WNAMES:
        common[nm] = np.ascontiguousarray(np.asarray(inputs[nm], np.float32))
    x = np.asarray(inputs["x"], np.float32)
    maps = []
    for b in range(x.shape[0]):
        m = dict(common)
        m["x"] = np.ascontiguousarray(x[b])
        maps.append(m)
    return maps


_PROG = {}
DEBUG_PHASES = None
DEBUG_DUMP = False
DEBUG_WIN_X = False
LAST_RES = None


def kernel(**inputs):
    x = np.asarray(inputs["x"])
    B, T, _ = x.shape
    if T not in _PROG:
        _PROG[T] = build_program(T)[0]
    nc = _PROG[T]
    maps = make_in_maps(inputs, T)
    res = run_bass_kernel_spmd(nc, maps, core_ids=list(range(B)))
    global LAST_RES
    LAST_RES = res
    out = np.stack([np.asarray(r["out"]) for r in res.results], axis=0)
    return out.astype(np.float32)
```

```python
import numpy as np
import concourse.bass as bass
import concourse.mybir as mybir
from concourse.bass_utils import run_bass_kernel_spmd

F32 = mybir.dt.float32
BF16 = mybir.dt.bfloat16
AF = mybir.ActivationFunctionType
ALU = mybir.AluOpType
AX = mybir.AxisListType

ENGS = ("pe", "act", "dve", "pool", "sp")
DMA_POOL = 8


class Res:
    __slots__ = ("name", "writer", "readers")

    def __init__(self, name=""):
        self.name = name
        self.writer = None
        self.readers = []


class Op:
    __slots__ = ("eng", "fn", "deps", "needed", "count", "is_dma", "dma_sem", "dma_target", "dma_prev", "gid")

    def __init__(self, eng, fn, is_dma):
        self.eng = eng
        self.fn = fn
        self.deps = []
        self.needed = False
        self.count = 0
        self.is_dma = is_dma
        self.dma_sem = None
        self.dma_target = 0
        self.dma_prev = 0
        self.gid = 0


class Sched:
    def __init__(self, nc):
        self.nc = nc
        self.ops = {e: [] for e in ENGS}
        self.n = 0
        self.dma_rr = {e: 0 for e in ENGS}
        self.dma_tgt = {}
        self.dma_since_barrier = []

    def add(self, eng, fn, reads=(), writes=(), dma=False):
        op = Op(eng, fn, dma)
        op.gid = self.n
        self.n += 1
        raw = set()
        other = set()
        for r in reads:
            if r.writer is not None:
                raw.add(r.writer)
        for w in writes:
            if w.writer is not None:
                other.add(w.writer)
            for rd in w.readers:
                other.add(rd)
        deps = set()
        for d in raw | other:
            if d is op:
                continue
            if d.is_dma or dma:
                deps.add(d)
            elif d.eng == eng:
                if eng == "pe":
                    continue
                deps.add(d)
            else:
                deps.add(d)
        op.deps = list(deps)
        for d in op.deps:
            d.needed = True
        for r in reads:
            r.readers.append(op)
        for w in writes:
            w.writer = op
            w.readers = []
        if dma:
            k = self.dma_rr[eng]
            self.dma_rr[eng] = (k + 1) % DMA_POOL
            key = (eng, k)
            prev = self.dma_tgt.get(key, 0)
            op.dma_sem = key
            op.dma_prev = prev
            op.dma_target = prev + 16
            self.dma_tgt[key] = prev + 16
            self.dma_since_barrier.append(op)
        self.ops[eng].append(op)
        return op

    def barrier(self):
        lasts = []
        for e in ENGS:
            for op in reversed(self.ops[e]):
                if not op.is_dma:
                    lasts.append(op)
                    break
        dmas = list(self.dma_since_barrier)
        self.dma_since_barrier = []
        for e in ENGS:
            op = Op(e, None, False)
            op.gid = self.n
            self.n += 1
            op.deps = [d for d in lasts if d.eng != e] + dmas
            for d in op.deps:
                d.needed = True
            self.ops[e].append(op)

    def emit(self):
        nc = self.nc
        for e in ENGS:
            c = 0
            for op in self.ops[e]:
                if op.is_dma or op.fn is None:
                    continue
                if op.needed:
                    c += 1
                    op.count = c
        import contextlib
        with contextlib.ExitStack() as st:
            esem = {e: st.enter_context(nc.semaphore("s_" + e)) for e in ENGS}
            dsem = {}
            for e in ("sp", "act", "pool"):
                for k in range(DMA_POOL):
                    dsem[(e, k)] = st.enter_context(nc.semaphore("d_%s%d" % (e, k)))
            block = st.enter_context(nc.Block())
            handles = {"pe": block.tensor, "act": block.scalar, "dve": block.vector,
                       "pool": block.gpsimd, "sp": block.sync}
            final_dma = dict(self.dma_tgt)

            def make(ename):
                def body(eng):
                    known = {}

                    def wait(sem_key, sem, val):
                        if val <= 0:
                            return
                        if known.get(sem_key, 0) >= val:
                            return
                        eng.wait_ge(sem, val)
                        known[sem_key] = val

                    for op in self.ops[ename]:
                        for d in op.deps:
                            if d.is_dma:
                                wait(d.dma_sem, dsem[d.dma_sem], d.dma_target)
                            else:
                                wait(d.eng, esem[d.eng], d.count)
                        if op.fn is None:
                            continue
                        if op.is_dma:
                            wait(op.dma_sem, dsem[op.dma_sem], op.dma_prev)
                            ins = op.fn(eng)
                            ins.then_inc(dsem[op.dma_sem], 16)
                        else:
                            ins = op.fn(eng)
                            if op.needed:
                                ins.then_inc(esem[ename], 1)
                    if ename == "sp":
                        for key, tgt in final_dma.items():
                            wait(key, dsem[key], tgt)
                return body

            for e in ENGS:
                handles[e](make(e))


class Arena:
    def __init__(self, ap_f32):
        self.t = ap_f32
        self.cap = ap_f32.shape[1] * 4
        self.off = 0
        self.peak = 0

    def alloc(self, shape, dtype, parts=128):
        n = int(np.prod(shape))
        sz = 2 if dtype == BF16 else 4
        nbytes = (n * sz + 63) // 64 * 64
        assert self.off + nbytes <= self.cap, ("arena overflow", self.off, nbytes, self.cap)
        v = self.t[0:parts, self.off // 4:(self.off + nbytes) // 4]
        self.off += nbytes
        self.peak = max(self.peak, self.off)
        if dtype == BF16:
            v = v.bitcast(BF16)
        v = v[:, 0:n]
        if len(shape) == 2:
            v = v.rearrange("p (a b) -> p a b", b=shape[1])
        elif len(shape) == 3:
            v = v.rearrange("p (a b c) -> p a b c", b=shape[1], c=shape[2])
        return v

    def mark(self):
        return self.off

    def release(self, m):
        self.off = m


D = 1024
DFF = 2816
NKC = D // 128
NFC = DFF // 128
EPS = 1e-6


class Ctx:
    pass


def cast_op(S, eng, out, in_, reads, writes):
    if eng == "act":
        return S.add("act", lambda e: e.copy(out, in_), reads, writes)
    return S.add(eng, lambda e: e.tensor_copy(out, in_), reads, writes)


def load_weight_rows(S, A, jobs):
    m = A.mark()
    NST = 3
    stg = [A.alloc((2816,), F32) for _ in range(NST)]
    r_stg = [Res("stg%d" % i) for i in range(NST)]
    engs = ("dve", "pool", "act")
    for j, (src, dst, rdst, shape) in enumerate(jobs):
        sb = j % NST
        n = int(np.prod(shape))
        sv = stg[sb][:, 0:n]
        if len(shape) == 2:
            sv = sv.rearrange("p (a b) -> p a b", b=shape[1])
        S.add("sp", (lambda e, sv=sv, src=src: e.dma_start(out=sv, in_=src)), [], [r_stg[sb]], dma=True)
        cast_op(S, engs[j % 3], dst, sv, [r_stg[sb]], [rdst])
    A.release(m)


def ffn_phase(S, A, P, K, x_src, x_dst, g_sb, wg_d, wu_d, wd_d, T, TT=256):
    m0 = A.mark()
    NS = TT // 128
    NT = T // TT
    wg = A.alloc((NKC, DFF), BF16)
    wu = A.alloc((NKC, DFF), BF16)
    wd = A.alloc((NFC, D), BF16)
    r_wg = [Res() for _ in range(NKC)]
    r_wu = [Res() for _ in range(NKC)]
    r_wd = [Res() for _ in range(NFC // 2)]
    jobs = []
    for kc in range(NKC):
        jobs.append((wg_d[kc * 128:(kc + 1) * 128, :], wg[:, kc, :], r_wg[kc], (DFF,)))
        jobs.append((wu_d[kc * 128:(kc + 1) * 128, :], wu[:, kc, :], r_wu[kc], (DFF,)))
    for c2 in range(NFC // 2):
        src = wd_d[c2 * 256:(c2 + 1) * 256, :].rearrange("(s p) d -> p s d", p=128)
        jobs.append((src, wd[:, 2 * c2:2 * c2 + 2, :], r_wd[c2], (2, D)))
    load_weight_rows(S, A, jobs)
    S.barrier()

    xt = [A.alloc((NS, D), F32) for _ in range(2)]
    r_xt = [Res() for _ in range(2)]
    hb = A.alloc((NS, D), BF16)
    r_hb = Res()
    hT = [A.alloc((NKC, TT), BF16) for _ in range(2)]
    r_hT = [[Res() for _ in range(NKC)] for _ in range(2)]
    aT = A.alloc((NFC, TT), BF16)
    r_aT = [Res() for _ in range(NFC)]
    sg = [A.alloc((TT,), F32) for _ in range(2)]
    r_sg = [Res() for _ in range(2)]
    junk = A.alloc((D,), BF16)
    r_junk = Res()
    ss = A.alloc((2 * NS,), F32)
    r_ss = Res()
    rs = A.alloc((2 * NS,), F32)
    r_rs = Res()
    sd = A.alloc((2 * NS,), F32)
    r_sd = Res()
    pT = [P.bf[0], P.bf[1]]
    r_pT = [P.r[0], P.r[1]]
    pG = [P.f[2], P.f[3]]
    r_pG = [P.r[2], P.r[3]]
    pU = [P.f[4], P.f[5]]
    r_pU = [P.r[4], P.r[5]]
    pO = [P.f[6], P.f[7]]
    r_pO = [P.r[6], P.r[7]]
    cnt = {"t": 0, "o": 0, "g": 0}

    def stageA(i):
        b = i % 2
        t0 = i * TT
        src = x_src[t0:t0 + TT, :].rearrange("(s p) d -> p s d", p=128)
        S.add("sp", lambda e: e.dma_start(out=xt[b], in_=src), [], [r_xt[b]], dma=True)
        for s in range(NS):
            S.add("act", lambda e, s=s: e.activation(out=junk, in_=xt[b][:, s, :], func=AF.Square,
                                                     accum_out=ss[:, b * NS + s:b * NS + s + 1]),
                  [r_xt[b]], [r_junk, r_ss])
        S.add("act", lambda e: e.activation(out=sd[:, b * NS:(b + 1) * NS], in_=ss[:, b * NS:(b + 1) * NS],
                                            func=AF.Sqrt, scale=1.0 / D, bias=K.eps_col),
              [r_ss, K.r_const], [r_sd])
        S.add("dve", lambda e: e.reciprocal(out=rs[:, b * NS:(b + 1) * NS], in_=sd[:, b * NS:(b + 1) * NS]),
              [r_sd], [r_rs])
        for s in range(NS):
            S.add("dve", lambda e, s=s: e.tensor_scalar(out=hb[:, s, :], in0=xt[b][:, s, :],
                                                        scalar1=rs[:, b * NS + s:b * NS + s + 1], scalar2=None,
                                                        op0=ALU.mult),
                  [r_xt[b], r_rs], [r_hb])
        for kc in range(NKC):
            q = cnt["t"] % 2
            cnt["t"] += 1
            for s in range(NS):
                S.add("pe", lambda e, s=s, kc=kc, q=q: e.transpose(out=pT[q][:, s * 128:(s + 1) * 128],
                                                                  in_=hb[:, s, kc * 128:(kc + 1) * 128],
                                                                  identity=K.ident),
                      [r_hb, K.r_const], [r_pT[q]])
            if kc % 2 == 0:
                S.add("dve", lambda e, kc=kc, q=q: e.tensor_scalar(out=hT[b][:, kc, :], in0=pT[q][:, 0:TT],
                                                                   scalar1=g_sb[:, kc:kc + 1], scalar2=None, op0=ALU.mult),
                      [r_pT[q], K.r_const], [r_hT[b][kc]])
            else:
                S.add("act", lambda e, kc=kc, q=q: e.activation(out=hT[b][:, kc, :], in_=pT[q][:, 0:TT], func=AF.Copy,
                                                                scale=g_sb[:, kc:kc + 1]),
                      [r_pT[q], K.r_const], [r_hT[b][kc]])

    def stageB1(i):
        b = i % 2
        for c in range(NFC):
            q = cnt["g"] % 2
            cnt["g"] += 1
            for kc in range(NKC):
                S.add("pe", lambda e, c=c, kc=kc, q=q: e.matmul(pG[q][:, 0:TT], lhsT=wg[:, kc, c * 128:(c + 1) * 128],
                                                               rhs=hT[b][:, kc, :], start=(kc == 0), stop=(kc == NKC - 1)),
                      [r_wg[kc], r_hT[b][kc]], [r_pG[q]])
            for kc in range(NKC):
                S.add("pe", lambda e, c=c, kc=kc, q=q: e.matmul(pU[q][:, 0:TT], lhsT=wu[:, kc, c * 128:(c + 1) * 128],
                                                               rhs=hT[b][:, kc, :], start=(kc == 0), stop=(kc == NKC - 1)),
                      [r_wu[kc], r_hT[b][kc]], [r_pU[q]])
            S.add("act", lambda e, q=q: e.activation(out=sg[q], in_=pG[q][:, 0:TT], func=AF.Silu),
                  [r_pG[q]], [r_sg[q]])
            S.add("dve", lambda e, c=c, q=q: e.tensor_tensor(out=aT[:, c, :], in0=pU[q][:, 0:TT], in1=sg[q], op=ALU.mult),
                  [r_pU[q], r_sg[q]], [r_aT[c]])

    def stageB2(i):
        b = i % 2
        t0 = i * TT
        for s in range(NS):
            for dh in range(2):
                q = cnt["o"] % 2
                cnt["o"] += 1
                for c in range(NFC):
                    S.add("pe", lambda e, c=c, s=s, dh=dh, q=q: e.matmul(pO[q], lhsT=aT[:, c, s * 128:(s + 1) * 128],
                                                                       rhs=wd[:, c, dh * 512:(dh + 1) * 512],
                                                                       start=(c == 0), stop=(c == NFC - 1)),
                          [r_aT[c], r_wd[c // 2]], [r_pO[q]])
                S.add("dve", lambda e, s=s, dh=dh, q=q: e.scalar_tensor_tensor(
                    out=xt[b][:, s, dh * 512:(dh + 1) * 512], in0=pO[q], scalar=0.5,
                    in1=xt[b][:, s, dh * 512:(dh + 1) * 512], op0=ALU.mult, op1=ALU.add),
                    [r_pO[q], r_xt[b]], [r_xt[b]])
        dst = x_dst[t0:t0 + TT, :].rearrange("(s p) d -> p s d", p=128)
        S.add("sp", lambda e: e.dma_start(out=dst, in_=xt[b]), [r_xt[b]], [], dma=True)

    stageA(0)
    for i in range(NT):
        stageB1(i)
        if i + 1 < NT:
            stageA(i + 1)
        stageB2(i)
    S.barrier()
    A.release(m0)


class Psum:
    pass


def setup_common(nc, st, arena_bytes):
    A = Arena(st.enter_context(nc.sbuf_tensor("arena", [128, arena_bytes // 4], F32)))
    P = Psum()
    P.f = []
    P.bf = []
    P.r = []
    for i in range(8):
        t = st.enter_context(nc.psum_tensor("ps%d" % i, [128, 512], F32))
        P.f.append(t[:, :])
        P.bf.append(t[:, :].bitcast(BF16))
        P.r.append(Res("ps%d" % i))
    return A, P


class NormT:
    def __init__(self, S, A, P, K, TT, nbuf_x=2, nbuf_h=2):
        self.S, self.P, self.K, self.TT = S, P, K, TT
        NS = TT // 128
        self.NS = NS
        self.xt = [A.alloc((NS, D), F32) for _ in range(nbuf_x)]
        self.r_xt = [Res() for _ in range(nbuf_x)]
        self.hb = A.alloc((NS, D), BF16)
        self.r_hb = Res()
        self.hT = [A.alloc((NKC, TT), BF16) for _ in range(nbuf_h)]
        self.r_hT = [[Res() for _ in range(NKC)] for _ in range(nbuf_h)]
        self.junk = A.alloc((D,), BF16)
        self.r_junk = Res()
        self.ss = A.alloc((2 * NS,), F32)
        self.r_ss = Res()
        self.rs = A.alloc((2 * NS,), F32)
        self.r_rs = Res()
        self.sd = A.alloc((2 * NS,), F32)
        self.r_sd = Res()
        self.tcnt = 0

    def run(self, i, x_src_tile, g_sb):
        S, P, K, TT, NS = self.S, self.P, self.K, self.TT, self.NS
        b = i % len(self.xt)
        hbuf = i % len(self.hT)
        xt, r_xt = self.xt[b], self.r_xt[b]
        hT, r_hT = self.hT[hbuf], self.r_hT[hbuf]
        ss, rs, sd, hb, junk = self.ss, self.rs, self.sd, self.hb, self.junk
        b2 = i % 2
        src = x_src_tile.rearrange("(s p) d -> p s d", p=128)
        S.add("sp", lambda e: e.dma_start(out=xt, in_=src), [], [r_xt], dma=True)
        for s in range(NS):
            S.add("act", lambda e, s=s: e.activation(out=junk, in_=xt[:, s, :], func=AF.Square,
                                                     accum_out=ss[:, b2 * NS + s:b2 * NS + s + 1]),
                  [r_xt], [self.r_junk, self.r_ss])
        S.add("act", lambda e: e.activation(out=sd[:, b2 * NS:(b2 + 1) * NS], in_=ss[:, b2 * NS:(b2 + 1) * NS],
                                            func=AF.Sqrt, scale=1.0 / D, bias=K.eps_col),
              [self.r_ss, K.r_const], [self.r_sd])
        S.add("dve", lambda e: e.reciprocal(out=rs[:, b2 * NS:(b2 + 1) * NS], in_=sd[:, b2 * NS:(b2 + 1) * NS]),
              [self.r_sd], [self.r_rs])
        for s in range(NS):
            S.add("dve", lambda e, s=s: e.tensor_scalar(out=hb[:, s, :], in0=xt[:, s, :],
                                                        scalar1=rs[:, b2 * NS + s:b2 * NS + s + 1], scalar2=None,
                                                        op0=ALU.mult),
                  [r_xt, self.r_rs], [self.r_hb])
        for kc in range(NKC):
            q = self.tcnt % 2
            self.tcnt += 1
            pT, r_pT = P.bf[q], P.r[q]
            for s in range(NS):
                S.add("pe", lambda e, s=s, kc=kc, pT=pT: e.transpose(out=pT[:, s * 128:(s + 1) * 128],
                                                                    in_=hb[:, s, kc * 128:(kc + 1) * 128],
                                                                    identity=K.ident),
                      [self.r_hb, K.r_const], [r_pT])
            if kc % 2 == 0:
                S.add("dve", lambda e, kc=kc, pT=pT: e.tensor_scalar(out=hT[:, kc, :], in0=pT[:, 0:TT],
                                                                     scalar1=g_sb[:, kc:kc + 1], scalar2=None, op0=ALU.mult),
                      [r_pT, K.r_const], [r_hT[kc]])
            else:
                S.add("act", lambda e, kc=kc, pT=pT: e.activation(out=hT[:, kc, :], in_=pT[:, 0:TT], func=AF.Copy,
                                                                  scale=g_sb[:, kc:kc + 1]),
                      [r_pT, K.r_const], [r_hT[kc]])
        return xt, r_xt, hT, r_hT


IN_COLS = 3072


def win_phase(S, A, P, K, x_src, g_sb, win_d, qg_sb, kg_sb, scr, T, TT=256):
    m0 = A.mark()
    NS = TT // 128
    NT = T // TT
    w = A.alloc((NKC, IN_COLS), BF16)
    r_w = [Res() for _ in range(NKC)]
    jobs = []
    for kc in range(NKC):
        for part in range(2):
            jobs.append((win_d[kc * 128:(kc + 1) * 128, part * 1536:(part + 1) * 1536],
                         w[:, kc, part * 1536:(part + 1) * 1536], r_w[kc], (1536,)))
    load_weight_rows(S, A, jobs)
    S.barrier()
    nt = NormT(S, A, P, K, TT, nbuf_x=1, nbuf_h=2)
    sq = [A.alloc((TT,), BF16) for _ in range(2)]
    r_sq = [Res() for _ in range(2)]
    zf = [A.alloc((TT,), F32) for _ in range(2)]
    r_zf = [Res() for _ in range(2)]
    rstd = [A.alloc((TT,), F32) for _ in range(2)]
    r_rstd = [Res() for _ in range(2)]
    qk_o = [A.alloc((TT,), BF16) for _ in range(3)]
    r_qk_o = [Res() for _ in range(3)]
    pl_o = [A.alloc((TT,), F32) for _ in range(2)]
    r_pl_o = [Res() for _ in range(2)]
    v_o = [A.alloc((1024,), BF16) for _ in range(2)]
    r_v_o = [Res() for _ in range(2)]
    for bi in range(2):
        S.add("pool", lambda e, bi=bi: e.memset(v_o[bi], 1.0), [], [r_v_o[bi]])
    h_o = [A.alloc((1280,), F32) for _ in range(2)]
    r_h_o = [Res() for _ in range(2)]
    cn = {"fm": 0, "qk": 0, "pl": 0, "v": 0, "h": 0, "tm": 0}
    for i in range(NT):
        t0 = i * TT
        xt, r_xt, hT, r_hT = nt.run(i, x_src[t0:t0 + TT, :], g_sb)
        for ci in range(10):
            col0 = ci * 128 if ci < 8 else 1536 + (ci - 8) * 128
            q = cn["fm"] % 2
            cn["fm"] += 1
            pz, r_pz = P.f[2 + q], P.r[2 + q]
            for kc in range(NKC):
                S.add("pe", lambda e, kc=kc, col0=col0, pz=pz, hT=hT: e.matmul(pz[:, 0:TT], lhsT=w[:, kc, col0:col0 + 128],
                                                                      rhs=hT[:, kc, :], start=(kc == 0), stop=(kc == NKC - 1)),
                      [r_w[kc], r_hT[kc]], [r_pz])
            if ci < 8:
                b = cn["qk"] % 2
                b3 = cn["qk"] % 3
                cn["qk"] += 1
                gains = qg_sb if ci < 4 else kg_sb
                gi = ci % 4
                S.add("act", lambda e, pz=pz, b=b: e.activation(out=sq[b], in_=pz[:, 0:TT], func=AF.Square),
                      [r_pz], [r_sq[b]])
                S.add("act", lambda e, pz=pz, b=b: e.copy(zf[b], pz[:, 0:TT]), [r_pz], [r_zf[b]])
                pm, r_pm = P.f[4], P.r[4]
                S.add("pe", lambda e, b=b: e.matmul(pm[:, 0:TT], lhsT=K.bd64, rhs=sq[b], start=True, stop=True),
                      [r_sq[b], K.r_const], [r_pm])
                S.add("act", lambda e, b=b, ci=ci: e.activation(out=rstd[b], in_=pm[:, 0:TT], func=AF.Sqrt,
                                                                scale=(64.0 if ci < 4 else 1.0),
                                                                bias=(K.eps64_col if ci < 4 else K.eps_col)),
                      [r_pm, K.r_const], [r_rstd[b]])
                S.add("dve", lambda e, b=b: e.reciprocal(out=rstd[b], in_=rstd[b]), [r_rstd[b]], [r_rstd[b]])
                S.add("dve", lambda e, b=b, b3=b3, gains=gains, gi=gi: e.scalar_tensor_tensor(
                    out=qk_o[b3], in0=zf[b], scalar=gains[:, gi:gi + 1], in1=rstd[b], op0=ALU.mult, op1=ALU.mult),
                    [r_zf[b], r_rstd[b], K.r_const], [r_qk_o[b3]])
                dst = (scr.qT if ci < 4 else scr.kT)[gi, :, t0:t0 + TT]
                S.add("pool", lambda e, b3=b3, dst=dst: e.dma_start(out=dst, in_=qk_o[b3]), [r_qk_o[b3]], [], dma=True)
            else:
                b = cn["pl"] % 2
                cn["pl"] += 1
                S.add("act", lambda e, pz=pz, b=b: e.copy(pl_o[b], pz[:, 0:TT]), [r_pz], [r_pl_o[b]])
                dst = scr.pT[ci - 8, :, t0:t0 + TT]
                S.add("pool", lambda e, b=b, dst=dst: e.dma_start(out=dst, in_=pl_o[b]), [r_pl_o[b]], [], dma=True)
        for s in range(NS):
            groups = [(1024, 512, "v", 0), (1792, 512, "h", 0), (2304, 512, "h", 512), (2816, 256, "h", 1024)]
            bv = cn["v"] % 2
            cn["v"] += 1
            for (c0, ncol, kind, off) in groups:
                q = cn["tm"] % 3
                cn["tm"] += 1
                pz, r_pz = P.f[5 + q], P.r[5 + q]
                for kc in range(NKC):
                    S.add("pe", lambda e, kc=kc, c0=c0, ncol=ncol, pz=pz, s=s, hT=hT: e.matmul(
                        pz[:, 0:ncol], lhsT=hT[:, kc, s * 128:(s + 1) * 128], rhs=w[:, kc, c0:c0 + ncol],
                        start=(kc == 0), stop=(kc == NKC - 1)),
                        [r_w[kc], r_hT[kc]], [r_pz])
                if kind == "v":
                    S.add("dve", lambda e, pz=pz, bv=bv: e.tensor_copy(
                        v_o[bv].rearrange("p (h c) -> p h c", c=128)[:, :, 0:64], pz[:, 0:512].rearrange("p (h c) -> p h c", c=64)),
                        [r_pz], [r_v_o[bv]])
                else:
                    eng = "act" if off == 512 else "dve"
                    if eng == "act":
                        S.add("act", lambda e, pz=pz, bv=bv, off=off, ncol=ncol: e.copy(h_o[bv][:, off:off + ncol], pz[:, 0:ncol]),
                              [r_pz], [r_h_o[bv]])
                    else:
                        S.add("dve", lambda e, pz=pz, bv=bv, off=off, ncol=ncol: e.tensor_copy(h_o[bv][:, off:off + ncol], pz[:, 0:ncol]),
                              [r_pz], [r_h_o[bv]])
            tok0 = t0 + s * 128
            S.add("pool", lambda e, bv=bv, tok0=tok0: e.dma_start(out=scr.v2[tok0:tok0 + 128, :], in_=v_o[bv]),
                  [r_v_o[bv]], [], dma=True)
            S.add("pool", lambda e, bv=bv, tok0=tok0: e.dma_start(out=scr.h[tok0:tok0 + 128, :], in_=h_o[bv]),
                  [r_h_o[bv]], [], dma=True)
    S.barrier()
    A.release(m0)


def wout_phase(S, A, P, K, x_src, x_dst, wout_d, scr, T, TT=512):
    m0 = A.mark()
    NS = TT // 128
    NT = T // TT
    w = A.alloc((NKC, D), BF16)
    r_w = [Res() for _ in range(NKC // 2)]
    jobs = []
    for k2 in range(NKC // 2):
        src = wout_d[k2 * 256:(k2 + 1) * 256, :].rearrange("(s p) d -> p s d", p=128)
        jobs.append((src, w[:, 2 * k2:2 * k2 + 2, :], r_w[k2], (2, D)))
    load_weight_rows(S, A, jobs)
    S.barrier()
    xt = [A.alloc((NS, D), F32) for _ in range(2)]
    r_xt = [Res() for _ in range(2)]
    yt = [A.alloc((NKC, TT), BF16) for _ in range(2)]
    r_yt = [Res() for _ in range(2)]
    oc = 0
    for i in range(NT):
        b = i % 2
        t0 = i * TT
        src = x_src[t0:t0 + TT, :].rearrange("(s p) d -> p s d", p=128)
        S.add("sp", lambda e, b=b, src=src: e.dma_start(out=xt[b], in_=src), [], [r_xt[b]], dma=True)
        ysrc = scr.yT[:, :, t0:t0 + TT].rearrange("k p t -> p k t")
        S.add("sp", lambda e, b=b, ysrc=ysrc: e.dma_start(out=yt[b], in_=ysrc), [], [r_yt[b]], dma=True)
        for s in range(NS):
            for dh in range(2):
                q = oc % 2
                oc += 1
                po, r_po = P.f[6 + q], P.r[6 + q]
                for kc in range(NKC):
                    S.add("pe", lambda e, kc=kc, s=s, dh=dh, po=po, b=b: e.matmul(
                        po, lhsT=yt[b][:, kc, s * 128:(s + 1) * 128], rhs=w[:, kc, dh * 512:(dh + 1) * 512],
                        start=(kc == 0), stop=(kc == NKC - 1)), [r_yt[b], r_w[kc // 2]], [r_po])
                S.add("dve", lambda e, s=s, dh=dh, po=po, b=b: e.tensor_tensor(
                    out=xt[b][:, s, dh * 512:(dh + 1) * 512], in0=po, in1=xt[b][:, s, dh * 512:(dh + 1) * 512], op=ALU.add),
                    [r_po, r_xt[b]], [r_xt[b]])
        dst = x_dst[t0:t0 + TT, :].rearrange("(s p) d -> p s d", p=128)
        S.add("sp", lambda e, b=b, dst=dst: e.dma_start(out=dst, in_=xt[b]), [r_xt[b]], [], dma=True)
    S.barrier()
    A.release(m0)


DILS = (1, 4, 16)
AT_DBG = set()
NEGM = -30000.0


def attn_phase(S, A, P, K, bt_d, mask_d, scr, T):
    m0 = A.mark()
    BTb = A.alloc((12, 512), BF16)
    r_BT = Res()
    m1 = A.mark()
    BTf = A.alloc((24, 256), F32)
    r_BTf = Res()
    mk = A.alloc((256,), F32)
    r_mk = Res()
    S.add("sp", lambda e: e.dma_start(out=mk, in_=mask_d), [], [r_mk], dma=True)
    for g in range(3):
        src = bt_d[g].rearrange("h p c -> p h c")
        S.add("sp", lambda e, g=g, src=src: e.dma_start(out=BTf[:, g * 8:(g + 1) * 8, :], in_=src), [], [r_BTf], dma=True)
    BTv = BTb.rearrange("p a (h c) -> p (a h) c", h=2)
    for gh in range(24):
        S.add("dve", lambda e, gh=gh: e.tensor_tensor(out=BTf[:, gh, :], in0=BTf[:, gh, :], in1=mk, op=ALU.add),
              [r_BTf, r_mk], [r_BTf])
        S.add("act", lambda e, gh=gh: e.activation(out=BTv[:, gh, :], in_=BTf[:, gh, :], func=AF.Exp), [r_BTf], [r_BT])
    S.barrier()
    A.release(m1)
    NB = 4
    qT = A.alloc((T,), BF16)
    kT = A.alloc((T,), BF16)
    r_qk = Res()
    vt = [A.alloc((T // 128, 256), BF16) for _ in range(2)]
    r_vt = [Res() for _ in range(2)]
    acc = A.alloc((2, T), F32)
    r_acc = Res()
    yo = A.alloc((T,), BF16, parts=64)
    r_yo = Res()
    rden = [A.alloc((512,), F32, parts=64) for _ in range(2)]
    r_rden = [Res() for _ in range(2)]
    E = [A.alloc((512,), BF16) for _ in range(NB)]
    r_E = [Res() for _ in range(NB)]
    E0 = [A.alloc((512,), BF16) for _ in range(NB)]
    r_E0 = [Res() for _ in range(NB)]
    cn = 0
    vcnt = 0
    for hp in range(4):
        S.add("sp", lambda e, hp=hp: e.dma_start(out=qT, in_=scr.qT[hp]), [], [r_qk], dma=True)
        S.add("sp", lambda e, hp=hp: e.dma_start(out=kT, in_=scr.kT[hp]), [], [r_qk], dma=True)
        if "nomemset" not in AT_DBG:
            S.add("pool", lambda e: e.memset(acc, 0.0), [], [r_acc])
        for g, d in enumerate(DILS):
            L = T // d
            nl = L // 128
            vb = vcnt % 2
            vcnt += 1
            vtg, r_vtg = vt[vb], r_vt[vb]
            vsrc = scr.v2[:, hp * 256:(hp + 1) * 256].rearrange("(n p r) c -> p r n c", p=128, r=d)
            vdst = vtg.rearrange("p (r n) c -> p r n c", r=d)
            for r in range(d):
                for n0 in range(0, nl, 8):
                    if "novt" in AT_DBG:
                        continue
                    n1 = min(nl, n0 + 8)
                    S.add("sp", lambda e, r=r, n0=n0, n1=n1, vsrc=vsrc, vdst=vdst: e.dma_start(
                        out=vdst[:, r, n0:n1, :], in_=vsrc[:, r, n0:n1, :]), [], [r_vtg], dma=True)
            btp = BTb[:, g * 4 + hp, :]
            btv = btp.rearrange("p (h c) -> p h c", h=2)
            pend = []
            LA = 2

            def back(item):
                (po, r_po, b, hw, nq, kts, r_, av) = item
                nk = len(kts)
                for hh in range(2):
                    for j, kt in enumerate(kts):
                        c0 = hh * hw + j * nq
                        S.add("pe", lambda e, po=po, j=j, kt=kt, b=b, r_=r_, nl=nl, hh=hh, nq=nq, nk=nk, c0=c0, vtg=vtg: e.matmul(
                            po[:, hh * 128:hh * 128 + nq], lhsT=vtg[:, r_ * nl + kt, hh * 128:(hh + 1) * 128],
                            rhs=E[b][:, c0:c0 + nq], start=(j == 0), stop=(j == nk - 1)),
                            [r_vtg, r_E[b]], [r_po])
                pv = po[:, 0:256].rearrange("p (a b) -> p a b", a=2)[:, :, 0:nq]
                S.add("dve", lambda e, av=av, pv=pv: e.tensor_tensor(out=av, in0=pv, in1=av, op=ALU.add),
                      [r_po, r_acc], [r_acc])

            for r in range(d):
                def cols(q0, n, r=r, d=d):
                    a = r + q0 * d
                    return slice(a, a + (n - 1) * d + 1, d)
                steps = [("first", 0, 64, [0])]
                for m in range(nl - 1):
                    steps.append(("main", 64 + 128 * m, 128, [m, m + 1]))
                steps.append(("last", L - 64, 64, [nl - 1]))
                for (kind, q0, nq, kts) in steps:
                    b = cn % NB
                    b3 = cn % 3
                    bo = cn % 2
                    cn += 1
                    psh = [P.f[2 * b3], P.f[2 * b3 + 1]]
                    r_psh = [P.r[2 * b3], P.r[2 * b3 + 1]]
                    po, r_po = P.f[6 + bo], P.r[6 + bo]
                    if kind == "main":
                        bsl = btp
                        hw = 256
                    else:
                        bsl = btv[:, :, 192:256] if kind == "first" else btv[:, :, 0:64]
                        hw = 64
                    for hh in range(2):
                        pr = slice(hh * 64, (hh + 1) * 64)
                        qs = qT[pr, cols(q0, nq)]
                        for j, kt in enumerate(kts):
                            ksl = kT[pr, cols(kt * 128, 128)]
                            c0 = j * nq
                            S.add("pe", lambda e, ps=psh[hh], ksl=ksl, qs=qs, c0=c0, nq=nq: e.matmul(
                                ps[:, c0:c0 + nq], lhsT=ksl, rhs=qs, start=True, stop=True), [r_qk], [r_psh[hh]])
                    for hh in range(2):
                        S.add("act", lambda e, ps=psh[hh], b=b, hw=hw, hh=hh: e.activation(out=E0[b][:, hh * hw:(hh + 1) * hw], in_=ps[:, 0:hw],
                                                                                   func=AF.Exp), [r_psh[hh]], [r_E0[b]])
                    if kind == "main":
                        S.add("pool", lambda e, b=b, bsl=bsl: e.tensor_tensor(out=E[b], in0=E0[b], in1=bsl, op=ALU.mult),
                              [r_E0[b], r_BT], [r_E[b]])
                    else:
                        S.add("pool", lambda e, b=b, bsl=bsl: e.tensor_tensor(
                            out=E[b][:, 0:128].rearrange("p (h c) -> p h c", h=2), in0=E0[b][:, 0:128].rearrange("p (h c) -> p h c", h=2),
                            in1=bsl, op=ALU.mult), [r_E0[b], r_BT], [r_E[b]])
                    pend.append((po, r_po, b, hw, nq, kts, r, acc[:, :, cols(q0, nq)]))
                    if len(pend) > LA:
                        back(pend.pop(0))
            while pend:
                back(pend.pop(0))
        for hh in range(2):
            for ci in range(T // 512):
                if "nofin" in AT_DBG:
                    continue
                b = cn % 4
                cn += 1
                ps, r_ps = P.f[b], P.r[b]
                cs = slice(ci * 512, (ci + 1) * 512)
                S.add("pe", lambda e, ps=ps, hh=hh, cs=cs: e.matmul(ps[0:64, :], lhsT=K.shift_f, rhs=acc[:, hh, cs], start=True, stop=True),
                      [r_acc, K.r_const], [r_ps])
                rb = ci % 2
                S.add("dve", lambda e, ps=ps, rb=rb: e.reciprocal(out=rden[rb], in_=ps[0:64, :]), [r_ps], [r_rden[rb]])
                S.add("dve", lambda e, hh=hh, cs=cs, rb=rb: e.tensor_tensor(out=yo[:, cs], in0=acc[0:64, hh, cs], in1=rden[rb], op=ALU.mult),
                      [r_acc, r_rden[rb]], [r_yo])
            S.add("sp", lambda e, hp=hp, hh=hh: e.dma_start(out=scr.yT[hp, hh * 64:(hh + 1) * 64, :], in_=yo),
                  [r_yo], [], dma=True)
    S.barrier()
    A.release(m0)


POOL_WINDOWS = (2, 4, 8, 16)


def pool_phase(S, A, P, K, pw_d, psc_sb, pinv_d, scr, T):
    m0 = A.mark()
    PADL = 16
    W = T + 32
    bufs = [A.alloc((W,), F32) for _ in range(3)]
    r_b = [Res() for _ in range(3)]
    R = A.alloc((T,), BF16)
    r_R = Res()
    yo = A.alloc((T,), BF16)
    r_yo = Res()
    wf = A.alloc((128,), F32)
    wb = A.alloc((128,), BF16)
    r_w = Res()
    pinv = A.alloc((16,), F32)
    r_pinv = Res()
    tmp = A.alloc((16,), F32)
    r_tmp = Res()
    for half in range(2):
        S.add("pool", lambda e: e.memset(wf, 0.0), [], [r_w])
        for j in range(2):
            S.add("sp", lambda e, j=j, half=half: e.dma_start(out=wf[j * 64:(j + 1) * 64, j * 64:(j + 1) * 64],
                                                             in_=pw_d[half * 2 + j]), [], [r_w], dma=True)
        S.add("dve", lambda e: e.tensor_copy(wb, wf), [r_w], [r_w])
        S.add("sp", lambda e, half=half: e.dma_start(out=pinv, in_=pinv_d[half]), [], [r_pinv], dma=True)
        for bi in range(3):
            S.add("pool", lambda e, bi=bi: e.memset(bufs[bi], 0.0), [], [r_b[bi]])
        S.add("sp", lambda e, half=half: e.dma_start(out=bufs[0][:, PADL:PADL + T], in_=scr.pT[half]), [], [r_b[0]], dma=True)
        for j in range(2):
            w = POOL_WINDOWS[half * 2 + j]
            pr = slice(j * 64, (j + 1) * 64)
            eng = "dve" if j == 0 else "pool"
            cur = 1
            S.add(eng, lambda e, pr=pr: e.tensor_tensor(out=bufs[1][pr, 1:W], in0=bufs[0][pr, 0:W - 1], in1=bufs[0][pr, 1:W], op=ALU.add),
                  [r_b[0]], [r_b[1]])
            sh = 1
            lo, hi = 1, W
            while 2 * sh < w:
                nxt = 2 if cur == 1 else 1
                lo2, hi2 = lo + sh, hi - sh
                S.add(eng, lambda e, pr=pr, cur=cur, nxt=nxt, lo2=lo2, hi2=hi2, sh=sh: e.tensor_tensor(
                    out=bufs[nxt][pr, lo2:hi2], in0=bufs[cur][pr, lo2 - sh:hi2 - sh], in1=bufs[cur][pr, lo2 + sh:hi2 + sh], op=ALU.add),
                    [r_b[cur]], [r_b[nxt]])
                cur = nxt
                lo, hi = lo2, hi2
                sh *= 2
            ww = bufs[cur][pr, PADL:PADL + T]
            uu = bufs[0][pr, PADL:PADL + T]
            S.add("dve", lambda e, pr=pr, ww=ww, uu=uu, w=w: e.scalar_tensor_tensor(out=R[pr, :], in0=ww, scalar=1.0 / w, in1=uu,
                                                                              op0=ALU.mult, op1=ALU.subtract),
                  [r_b[cur], r_b[0]], [r_R])
            for (c0, p0) in ((0, 0), (T - 8, 8)):
                S.add(eng, lambda e, pr=pr, ww=ww, c0=c0, p0=p0: e.tensor_tensor(out=tmp[pr, p0:p0 + 8], in0=ww[:, c0:c0 + 8],
                                                                             in1=pinv[pr, p0:p0 + 8], op=ALU.mult),
                      [r_b[cur], r_pinv], [r_tmp])
                S.add(eng, lambda e, pr=pr, uu=uu, c0=c0, p0=p0: e.tensor_tensor(out=R[pr, c0:c0 + 8], in0=tmp[pr, p0:p0 + 8],
                                                                             in1=uu[:, c0:c0 + 8], op=ALU.subtract),
                      [r_tmp, r_b[0], r_R], [r_R])
        for ti in range(T // 512):
            q = ti % 2
            ps, r_ps = P.f[q], P.r[q]
            S.add("pe", lambda e, ps=ps, ti=ti: e.matmul(ps, lhsT=wb, rhs=R[:, ti * 512:(ti + 1) * 512], start=True, stop=True),
                  [r_w, r_R], [r_ps])
            S.add("act", lambda e, ps=ps, ti=ti, half=half: e.activation(out=yo[:, ti * 512:(ti + 1) * 512], in_=ps, func=AF.Copy,
                                                                      scale=psc_sb[:, half:half + 1]),
                  [r_ps, K.r_const], [r_yo])
        S.add("sp", lambda e, half=half: e.dma_start(out=scr.yT[4 + half], in_=yo), [r_yo], [], dma=True)
    S.barrier()
    A.release(m0)


HG_SKIP = set()


def hgrn_phase(S, A, P, K, layer, lbl_d, gn_sb, hc_d, scr, T):
    m0 = A.mark()
    NTL = T // 128
    C = 32
    tri = A.alloc((2, 128), F32)
    sel = A.alloc((2, 4), F32)
    msk = A.alloc((2, 512), F32)
    rmask = A.alloc((4,), F32)
    r_c = Res()
    S.add("sp", lambda e: e.dma_start(out=rmask, in_=hc_d["rmask"]), [], [r_c], dma=True)
    S.add("sp", lambda e: e.dma_start(out=tri, in_=hc_d["tri"].rearrange("d p c -> p d c")), [], [r_c], dma=True)
    S.add("sp", lambda e: e.dma_start(out=sel, in_=hc_d["sel"].rearrange("d p c -> p d c")), [], [r_c], dma=True)
    S.add("sp", lambda e: e.dma_start(out=msk, in_=hc_d["mask"].rearrange("d p c -> p d c")), [], [r_c], dma=True)
    oml = A.alloc((2, 256), F32)
    r_oml = Res()
    if layer == 0:
        S.add("pool", lambda e: e.memset(oml, 1.0), [], [r_oml])
    else:
        lrow = A.alloc((2, 2, 256), F32, parts=1)
        drow = A.alloc((2, 256), F32, parts=1)
        r_l = Res()
        S.add("sp", lambda e: e.dma_start(out=lrow, in_=lbl_d.rearrange("(o a) b c -> o a b c", o=1)), [], [r_l], dma=True)
        S.add("dve", lambda e: e.tensor_tensor(out=drow, in0=lrow[:, :, 1, :], in1=lrow[:, :, 0, :], op=ALU.subtract), [r_l], [r_l])
        pb, r_pb = P.f[0], P.r[0]
        S.add("pe", lambda e: e.matmul(pb, lhsT=K.ones_f[0:1, :], rhs=drow.rearrange("p a b -> p (a b)"), start=True, stop=True),
              [r_l, K.r_const], [r_pb])
        S.add("act", lambda e: e.activation(out=oml.rearrange("p a b -> p (a b)"), in_=pb, func=AF.Sigmoid, scale=-1.0),
              [r_pb], [r_oml])
    nb = 3
    hz = [A.alloc((1280,), F32) for _ in range(nb)]
    r_hz = [Res() for _ in range(nb)]

    def mk(shape, dt, parts=128):
        return [A.alloc(shape, dt, parts) for _ in range(nb)], [Res() for _ in range(nb)]
    sgm, r_sgm = mk((256,), F32)
    kk, r_kk = mk((256,), F32)
    ff, r_ff = mk((256,), F32)
    G, r_G = mk((256,), F32)
    EBm, r_EBm = mk((256,), F32)
    EBp, r_EBp = mk((256,), F32)
    Kt, r_Kt = mk((256,), BF16)
    KtM, r_KtM = mk((4, 256), BF16)
    Qt, r_Qt = mk((256,), BF16)
    Vb, r_Vb = mk((256,), BF16)
    QT, r_QT = mk((4, 128), BF16, 64)
    KT, r_KT = mk((4, 128), BF16, 64)
    a_sb, r_a = mk((4, 4), F32, 64)
    attm, r_attm = mk((512,), BF16)
    Sbf, r_Sbf = mk((4, 256), BF16, 64)
    tmp, r_tmp = mk((256,), F32, 64)
    osb, r_osb = mk((512,), F32, 64)
    obl, r_obl = mk((512,), F32, 64)
    sqo, r_sqo = mk((512,), F32, 64)
    rst, r_rst = mk((512,), F32, 64)
    sgt, r_sgt = mk((256,), BF16)
    eg, r_eg = mk((256,), F32)
    t1, r_t1 = mk((512,), F32, 64)
    yo, r_yo = mk((4, 128), BF16, 64)
    St = A.alloc((256,), F32, parts=64)
    r_St = Res()
    pB, r_pB = P.f[0], P.r[0]
    pq, r_pq = P.bf[1], P.r[1]
    pk, r_pk = P.bf[2], P.r[2]
    pu = [P.f[3], P.f[4]]
    r_pu = [P.r[3], P.r[4]]
    pa2, r_pa2 = P.f[5], P.r[5]
    po, r_po = P.f[6], P.r[6]
    pms, r_pms = P.f[7], P.r[7]
    for (dd, order) in ((1, list(range(NTL - 1, -1, -1))), (0, list(range(NTL)))):
        S.add("pool", lambda e: e.memset(St, 0.0), [], [r_St])
        def front(it, ti):
            b = it % nb
            tok0 = ti * 128
            S.add("sp", lambda e, b=b, tok0=tok0: e.dma_start(out=hz[b], in_=scr.h[tok0:tok0 + 128, :]), [], [r_hz[b]], dma=True)
            hq = hz[b][:, 0:256]
            hi = hz[b][:, 256:512]
            hf = hz[b][:, 512 + 256 * dd:768 + 256 * dd]
            hg = hz[b][:, 1024:1280]
            S.add("act", lambda e, b=b, hf=hf: e.activation(out=sgm[b], in_=hf, func=AF.Exp), [r_hz[b]], [r_sgm[b]])
            S.add("dve", lambda e, b=b: e.tensor_scalar(out=sgm[b], in0=sgm[b], scalar1=1.0, scalar2=None, op0=ALU.add),
                  [r_sgm[b]], [r_sgm[b]])
            S.add("dve", lambda e, b=b: e.reciprocal(out=sgm[b], in_=sgm[b]), [r_sgm[b]], [r_sgm[b]])
            S.add("dve", lambda e, b=b, dd=dd: e.tensor_tensor(out=kk[b], in0=sgm[b], in1=oml[:, dd, :], op=ALU.mult),
                  [r_sgm[b], r_oml], [r_kk[b]])
            S.add("dve", lambda e, b=b: e.tensor_scalar(out=ff[b], in0=kk[b], scalar1=-1.0, scalar2=1.0, op0=ALU.mult, op1=ALU.add),
                  [r_kk[b]], [r_ff[b]])
            S.add("act", lambda e, b=b: e.activation(out=G[b], in_=ff[b], func=AF.Ln), [r_ff[b]], [r_G[b]])
            if "tri" not in HG_SKIP:
              S.add("pe", lambda e, b=b, dd=dd: e.matmul(pB[:, 0:256], lhsT=tri[:, dd, :], rhs=G[b], start=True, stop=True),
                  [r_G[b], r_c], [r_pB])
            S.add("act", lambda e, b=b: e.activation(out=EBm[b], in_=pB[:, 0:256], func=AF.Exp, scale=-1.0), [r_pB], [r_EBm[b]])
            S.add("act", lambda e, b=b: e.activation(out=EBp[b], in_=pB[:, 0:256], func=AF.Exp), [r_pB], [r_EBp[b]])
            S.add("dve", lambda e, b=b: e.tensor_tensor(out=Kt[b], in0=kk[b], in1=EBm[b], op=ALU.mult), [r_kk[b], r_EBm[b]], [r_Kt[b]])
            S.add("pool", lambda e, b=b, hq=hq: e.tensor_tensor(out=Qt[b], in0=hq, in1=EBp[b], op=ALU.mult), [r_hz[b], r_EBp[b]], [r_Qt[b]])
            S.add("pool", lambda e, b=b, hi=hi: e.tensor_copy(Vb[b], hi), [r_hz[b]], [r_Vb[b]])
            for h in range(4):
                if "tr" in HG_SKIP:
                    continue
                S.add("pe", lambda e, b=b, h=h: e.transpose(out=pq[0:64, h * 128:(h + 1) * 128], in_=Qt[b][:, h * 64:(h + 1) * 64],
                                                            identity=K.ident), [r_Qt[b], K.r_const], [r_pq])
            for h in range(4):
                if "tr" in HG_SKIP:
                    continue
                S.add("pe", lambda e, b=b, h=h: e.transpose(out=pk[0:64, h * 128:(h + 1) * 128], in_=Kt[b][:, h * 64:(h + 1) * 64],
                                                            identity=K.ident), [r_Kt[b], K.r_const], [r_pk])
            S.add("act", lambda e, b=b: e.copy(QT[b].rearrange("p a b -> p (a b)"), pq[0:64, 0:512]), [r_pq], [r_QT[b]])
            S.add("dve", lambda e, b=b: e.tensor_copy(KT[b].rearrange("p a b -> p (a b)"), pk[0:64, 0:512]), [r_pk], [r_KT[b]])
            for h in range(4):
                if "a" in HG_SKIP:
                    continue
                S.add("pe", lambda e, b=b, h=h, dd=dd: e.matmul(pB[0:64, 256 + h * 4:256 + (h + 1) * 4], lhsT=EBp[b][:, h * 64:(h + 1) * 64],
                                                               rhs=sel[:, dd, :], start=True, stop=True), [r_EBp[b], r_c], [r_pB])
            S.add("dve", lambda e, b=b: e.tensor_copy(a_sb[b].rearrange("p a b -> p (a b)"), pB[0:64, 256:272]), [r_pB], [r_a[b]])
            for j in range(4):
                S.add("pool", lambda e, b=b, j=j: e.tensor_scalar(out=KtM[b][:, j, :], in0=Kt[b], scalar1=rmask[:, j:j + 1], scalar2=None,
                                                                  op0=ALU.mult), [r_Kt[b], r_c], [r_KtM[b]])
        def back(it, ti):
            b = it % nb
            tok0 = ti * 128
            hg = hz[b][:, 1024:1280]
            for j in range(4):
                for h in range(4):
                    if "u" in HG_SKIP:
                        continue
                    S.add("pe", lambda e, b=b, j=j, h=h: e.matmul(
                        pu[j // 2][0:64, ((j % 2) * 4 + h) * 64:((j % 2) * 4 + h + 1) * 64],
                        lhsT=KtM[b][:, j, h * 64:(h + 1) * 64], rhs=Vb[b][:, h * 64:(h + 1) * 64],
                        start=True, stop=True), [r_KtM[b], r_Vb[b]], [r_pu[j // 2]])
            jorder = (0, 1, 2, 3) if dd == 0 else (3, 2, 1, 0)
            for j in jorder:
                S.add("act", lambda e, b=b, j=j: e.copy(Sbf[b][:, j, :], St), [r_St], [r_Sbf[b]])
                S.add("dve", lambda e, b=b, j=j: e.tensor_tensor(out=tmp[b], in0=pu[j // 2][0:64, (j % 2) * 256:(j % 2 + 1) * 256], in1=St,
                                                                 op=ALU.add), [r_pu[j // 2], r_St], [r_tmp[b]])
                for h in range(4):
                    S.add("dve", lambda e, b=b, j=j, h=h: e.tensor_scalar(out=St[:, h * 64:(h + 1) * 64], in0=tmp[b][:, h * 64:(h + 1) * 64],
                                                                          scalar1=a_sb[b][:, h, j:j + 1], scalar2=None, op0=ALU.mult),
                          [r_tmp[b], r_a[b]], [r_St])
            for h in range(4):
                S.add("pe", lambda e, b=b, h=h: e.matmul(pa2[:, h * 128:(h + 1) * 128], lhsT=KT[b][:, h, :], rhs=QT[b][:, h, :],
                                                         start=True, stop=True), [r_KT[b], r_QT[b]], [r_pa2])
            S.add("dve", lambda e, b=b, dd=dd: e.tensor_tensor(out=attm[b], in0=pa2, in1=msk[:, dd, :], op=ALU.mult),
                  [r_pa2, r_c], [r_attm[b]])
            for h in range(4):
                S.add("pe", lambda e, b=b, h=h: e.matmul(po[0:64, h * 128:(h + 1) * 128], lhsT=Vb[b][:, h * 64:(h + 1) * 64],
                                                         rhs=attm[b][:, h * 128:(h + 1) * 128], start=True, stop=False),
                      [r_Vb[b], r_attm[b]], [r_po])
                for j in range(4):
                    S.add("pe", lambda e, b=b, h=h, j=j: e.matmul(po[0:64, h * 128 + 32 * j:h * 128 + 32 * j + 32],
                                                                 lhsT=Sbf[b][:, j, h * 64:(h + 1) * 64], rhs=QT[b][:, h, 32 * j:32 * j + 32],
                                                                 start=False, stop=(j == 3)), [r_Sbf[b], r_QT[b]], [r_po])
            obv = scr.ob[:, :, tok0:tok0 + 128]
            if dd == 1:
                S.add("act", lambda e, b=b: e.copy(osb[b], po[0:64, :]), [r_po], [r_osb[b]])
                S.add("pool", lambda e, b=b, obv=obv: e.dma_start(out=obv, in_=osb[b].rearrange("p (a b) -> p a b", a=4)),
                      [r_osb[b]], [], dma=True)
            else:
                S.add("sp", lambda e, b=b, obv=obv: e.dma_start(out=obl[b].rearrange("p (a b) -> p a b", a=4), in_=obv), [], [r_obl[b]], dma=True)
                S.add("dve", lambda e, b=b: e.tensor_tensor(out=osb[b], in0=po[0:64, :], in1=obl[b], op=ALU.add), [r_po, r_obl[b]], [r_osb[b]])
                S.add("act", lambda e, b=b: e.activation(out=sqo[b], in_=osb[b], func=AF.Square), [r_osb[b]], [r_sqo[b]])
                S.add("pe", lambda e, b=b: e.matmul(pms[0:64, :], lhsT=K.mean64_f, rhs=sqo[b], start=True, stop=True),
                      [r_sqo[b], K.r_const], [r_pms])
                S.add("act", lambda e, b=b: e.activation(out=rst[b], in_=pms[0:64, :], func=AF.Ln, scale=1.0, bias=K.eps_col[0:64, :]),
                      [r_pms, K.r_const], [r_rst[b]])
                S.add("act", lambda e, b=b: e.activation(out=rst[b], in_=rst[b], func=AF.Exp, scale=-0.5), [r_rst[b]], [r_rst[b]])
                S.add("act", lambda e, b=b, hg=hg: e.activation(out=eg[b], in_=hg, func=AF.Exp, scale=-1.0), [r_hz[b]], [r_eg[b]])
                S.add("pool", lambda e, b=b: e.tensor_scalar(out=eg[b], in0=eg[b], scalar1=1.0, scalar2=None, op0=ALU.add),
                      [r_eg[b]], [r_eg[b]])
                S.add("dve", lambda e, b=b: e.reciprocal(out=eg[b], in_=eg[b]), [r_eg[b]], [r_eg[b]])
                S.add("pool", lambda e, b=b, hg=hg: e.tensor_tensor(out=sgt[b], in0=hg, in1=eg[b], op=ALU.mult),
                      [r_hz[b], r_eg[b]], [r_sgt[b]])
                for h in range(4):
                    S.add("pe", lambda e, b=b, h=h: e.transpose(out=pq[0:64, 512 + h * 128:512 + (h + 1) * 128], in_=sgt[b][:, h * 64:(h + 1) * 64],
                                                                identity=K.ident), [r_sgt[b], K.r_const], [r_pq])
                S.add("dve", lambda e, b=b: e.scalar_tensor_tensor(out=t1[b], in0=osb[b], scalar=gn_sb[:, 0:1], in1=rst[b],
                                                                   op0=ALU.mult, op1=ALU.mult), [r_osb[b], r_rst[b], K.r_const], [r_t1[b]])
                S.add("dve", lambda e, b=b: e.tensor_tensor(out=yo[b].rearrange("p a b -> p (a b)"), in0=t1[b], in1=pq[0:64, 512:1024], op=ALU.mult),
                      [r_t1[b], r_pq], [r_yo[b]])
                for h in range(4):
                    S.add("pool", lambda e, b=b, h=h, tok0=tok0: e.dma_start(
                        out=scr.yT[6 + h // 2, (h % 2) * 64:(h % 2) * 64 + 64, tok0:tok0 + 128], in_=yo[b][:, h, :]),
                        [r_yo[b]], [], dma=True)
        front(0, order[0])
        for it, ti in enumerate(order):
            if it + 1 < len(order):
                front(it + 1, order[it + 1])
            back(it, ti)
        S.barrier()
    A.release(m0)


DEPTH = 2
NUM_BUCKETS = 32
MAX_DISTANCE = 1024


def _t5_bucket_np(rel):
    half = NUM_BUCKETS // 2
    max_exact = half // 2
    base = np.where(rel > 0, half, 0)
    n = np.abs(rel)
    nf = np.maximum(n, 1).astype(np.float32)
    large = max_exact + (np.log(nf / np.float32(max_exact)) / np.float32(np.log(MAX_DISTANCE / max_exact))
                         * np.float32(half - max_exact)).astype(np.int32)
    large = np.minimum(large, half - 1)
    return base + np.where(n < max_exact, n, large)


def _attn_rel():
    j = np.arange(128)[:, None]
    c = np.arange(256)[None, :]
    rel = np.where(c < 128, j - c - 64, j - (c - 128) + 64)
    return rel


def host_constants(T):
    rel = _attn_rel()
    valid = np.abs(rel) <= 64
    c = {}
    c["amask"] = np.where(valid, 0.0, NEGM).astype(np.float32)
    c["abkt"] = np.stack([np.where(valid, _t5_bucket_np(rel * d), 0) for d in DILS])
    c["avalid"] = valid
    mats = np.zeros((128, 128 * 3 + 64 + 128), np.float32)
    mats[:, 0:128] = np.eye(128)
    bd = np.zeros((128, 128), np.float32)
    bd[0:64, 0:64] = 1.0 / 64
    bd[64:128, 64:128] = 1.0 / 64
    mats[:, 128:256] = bd
    mats[:, 256:384] = 1.0
    mats[0:64, 384:448] = 1.0 / 64
    for i in range(64):
        mats[64 + i, 448 + i] = 1.0
    c["mats"] = mats
    s = np.arange(128)[:, None]
    t = np.arange(128)[None, :]
    same = (s // 32) == (t // 32)
    tri = np.stack([(same & (s <= t)), (same & (s >= t))]).astype(np.float32)
    sel = np.zeros((2, 128, 4), np.float32)
    for j in range(4):
        sel[0, 32 * j + 31, j] = 1.0
        sel[1, 32 * j, j] = 1.0
    c["h_tri"] = tri
    c["h_sel"] = sel
    c["h_rmask"] = ((np.arange(128)[:, None] // 32) == np.arange(4)[None, :]).astype(np.float32)
    c["h_mask"] = np.ascontiguousarray(np.tile(tri, (1, 1, 4)))
    pinv = np.zeros((2, 128, 16), np.float32)
    for half in range(2):
        for p in range(128):
            w = POOL_WINDOWS[half * 2 + p // 64]
            for k in range(8):
                for (pos, col) in ((k, k), (T - 8 + k, 8 + k)):
                    lo = max(pos - w // 2, 0)
                    hi = min(pos + w - w // 2, T)
                    pinv[half, p, col] = 1.0 / (hi - lo)
    c["pinv"] = pinv
    return c


NVEC = 6 * 8 + 2 * 4 + 2 * 4 + 2 * 2 + 2


def layout_vecs(inp):
    v = np.zeros((128, NVEC), np.float32)
    o = 0
    for name in ("ffn1_norm", "mix_norm", "ffn2_norm"):
        for l in range(DEPTH):
            v[:, o:o + 8] = np.asarray(inp[name][l], np.float32).reshape(8, 128).T
            o += 8
    for name in ("q_norm", "k_norm"):
        for l in range(DEPTH):
            v[:, o:o + 4] = np.asarray(inp[name][l], np.float32).reshape(4, 128).T
            o += 4
    for l in range(DEPTH):
        v[:, o:o + 2] = np.asarray(inp["pool_scale"][l], np.float32).reshape(2, 128).T
        o += 2
    for l in range(DEPTH):
        v[0:64, o] = np.asarray(inp["hgrn_norm"][l], np.float32)
        o += 1
    return v


def build_program(T):
    import contextlib
    nc = bass.Bass("TRN2", target_bir_lowering=False)
    dt = lambda name, shape, dtype=F32, kind="ExternalInput": nc.dram_tensor(name, list(shape), dtype, kind=kind).ap()
    x_d = dt("x", [T, D])
    vec_d = dt("vecs", [128, NVEC])
    mats_d = dt("mats", [128, 576])
    w = {}
    for nm in ("ffn1", "ffn2"):
        w[nm + "_w_gate"] = dt(nm + "_w_gate", [DEPTH, D, DFF])
        w[nm + "_w_up"] = dt(nm + "_w_up", [DEPTH, D, DFF])
        w[nm + "_w_down"] = dt(nm + "_w_down", [DEPTH, DFF, D])
    w["w_in"] = dt("w_in", [DEPTH, D, IN_COLS])
    w["w_out"] = dt("w_out", [DEPTH, D, D])
    w["pool_w"] = dt("pool_w", [DEPTH, 4, 64, 64])
    lbl_d = dt("hgrn_lb_logits", [2, DEPTH, 256])
    abias_d = dt("abias", [3, 8, 128, 256])
    amask_d = dt("amask", [128, 256])
    pinv_d = dt("pinv", [2, 128, 16])
    hc = {"tri": dt("h_tri", [2, 128, 128]), "sel": dt("h_sel", [2, 128, 4]), "mask": dt("h_mask", [2, 128, 512]),
          "rmask": dt("h_rmask", [128, 4])}
    out_d = dt("out", [T, D], F32, "ExternalOutput")
    scr = Ctx()
    IK = "ExternalOutput" if DEBUG_DUMP else "Internal"
    scr.xa = dt("s_xa", [T, D], F32, IK)
    scr.qT = dt("s_qT", [4, 128, T], BF16, IK)
    scr.kT = dt("s_kT", [4, 128, T], BF16, IK)
    scr.v2 = dt("s_v", [T, 1024], BF16, IK)
    scr.pT = dt("s_pT", [2, 128, T], F32, IK)
    scr.h = dt("s_h", [T, 1280], F32, IK)
    scr.ob = dt("s_ob", [64, 4, T], F32, IK)
    scr.of = dt("s_of", [64, 4, T], F32, IK)
    scr.yT = dt("s_yT", [8, 128, T], BF16, IK)
    with contextlib.ExitStack() as st:
        A, P = setup_common(nc, st, 207 * 1024)
        S = Sched(nc)
        K = Ctx()
        K.r_const = Res("const")
        vecs = A.alloc((NVEC,), F32)
        matf = A.alloc((576,), F32)
        r_mf = Res()
        S.add("sp", lambda e: e.dma_start(out=vecs, in_=vec_d), [], [K.r_const], dma=True)
        S.add("sp", lambda e: e.dma_start(out=matf, in_=mats_d), [], [r_mf], dma=True)
        K.ident = A.alloc((128,), BF16)
        K.bd64 = A.alloc((128,), BF16)
        K.ones_bf = A.alloc((128,), BF16)
        S.add("dve", lambda e: e.tensor_copy(K.ident, matf[:, 0:128]), [r_mf], [K.r_const])
        S.add("dve", lambda e: e.tensor_copy(K.bd64, matf[:, 128:256]), [r_mf], [K.r_const])
        S.add("dve", lambda e: e.tensor_copy(K.ones_bf, matf[:, 256:384]), [r_mf], [K.r_const])
        K.ones_f = matf[:, 256:384]
        K.mean64_f = matf[0:64, 384:448]
        K.shift_f = matf[:, 448:512]
        K.eps_col = A.alloc((1,), F32)
        K.eps64_col = A.alloc((1,), F32)
        S.add("pool", lambda e: e.memset(K.eps_col, EPS), [], [K.r_const])
        S.add("pool", lambda e: e.memset(K.eps64_col, 64.0 * EPS), [], [K.r_const])
        S.add("dve", lambda e: e.tensor_copy(K.eps_col, K.eps_col), [r_mf, K.r_const], [K.r_const])
        S.barrier()

        def vcol(o, n):
            return vecs[:, o:o + n]
        for l in range(DEPTH):
            g1 = vcol(0 + 8 * l, 8)
            gm = vcol(16 + 8 * l, 8)
            g2 = vcol(32 + 8 * l, 8)
            qg = vcol(48 + 4 * l, 4)
            kg = vcol(56 + 4 * l, 4)
            psc = vcol(64 + 2 * l, 2)
            gn = vecs[0:64, 68 + l:69 + l]
            src = x_d if l == 0 else scr.xa
            dst = out_d if l == DEPTH - 1 else scr.xa
            steps = [
                lambda: ffn_phase(S, A, P, K, src, scr.xa, g1, w["ffn1_w_gate"][l], w["ffn1_w_up"][l], w["ffn1_w_down"][l], T),
                lambda: win_phase(S, A, P, K, (x_d if DEBUG_WIN_X else scr.xa), gm, w["w_in"][l], qg, kg, scr, T),
                lambda: attn_phase(S, A, P, K, abias_d, amask_d, scr, T),
                lambda: pool_phase(S, A, P, K, w["pool_w"][l], psc, pinv_d, scr, T),
                lambda: hgrn_phase(S, A, P, K, l, lbl_d, gn, hc, scr, T),
                lambda: wout_phase(S, A, P, K, scr.xa, scr.xa, w["w_out"][l], scr, T),
                lambda: ffn_phase(S, A, P, K, scr.xa, dst, g2, w["ffn2_w_gate"][l], w["ffn2_w_up"][l], w["ffn2_w_down"][l], T),
            ]
            for si, stp in enumerate(steps):
                if DEBUG_PHASES is None or (l * 7 + si) in DEBUG_PHASES:
                    stp()
        S.emit()
    return nc, S


_WNAMES = ("ffn1_w_gate", "ffn1_w_up", "ffn1_w_down", "ffn2_w_gate", "ffn2_w_up", "ffn2_w_down",
           "w_in", "w_out", "pool_w", "hgrn_lb_logits")


def make_in_maps(inputs, T):
    c = host_constants(T)
    rb = np.asarray(inputs["rel_bias"], np.float32)
    ab = np.zeros((3, 8, 128, 256), np.float32)
    for g in range(3):
        gathered = rb[c["abkt"][g]]
        gathered = np.where(c["avalid"][:, :, None], gathered, np.float32(0.0))
        ab[g] = gathered.transpose(2, 0, 1)
    common = {"vecs": layout_vecs(inputs), "mats": c["mats"], "abias": ab, "amask": c["amask"], "pinv": c["pinv"],
              "h_tri": c["h_tri"], "h_sel": c["h_sel"], "h_mask": c["h_mask"], "h_rmask": c["h_rmask"]}
    for nm in _WNAMES:
        common[nm] = np.ascontiguousarray(np.asarray(inputs[nm], np.float32))
    x = np.asarray(inputs["x"], np.float32)
    maps = []
    for b in range(x.shape[0]):
        m = dict(common)
        m["x"] = np.ascontiguousarray(x[b])
        maps.append(m)
    return maps


_PROG = {}
DEBUG_PHASES = None
DEBUG_DUMP = False
DEBUG_WIN_X = False
LAST_RES = None


def kernel(**inputs):
    x = np.asarray(inputs["x"])
    B, T, _ = x.shape
    if T not in _PROG:
        _PROG[T] = build_program(T)[0]
    nc = _PROG[T]
    maps = make_in_maps(inputs, T)
    res = run_bass_kernel_spmd(nc, maps, core_ids=list(range(B)))
    global LAST_RES
    LAST_RES = res
    out = np.stack([np.asarray(r["out"]) for r in res.results], axis=0)
    return out.astype(np.float32)
```
